# Optimizing a Trainium2 kernel written in Bass

```python
import math
import jax, jax.numpy as jnp
from jax import lax
import numpy as np

D_MODEL = 2048
BATCH = 2
SEQ = 16384
DEPTH = 2

D_MIX = D_MODEL
NSA_HEADS = 16
NSA_KV_HEADS = 4
NSA_HEAD_DIM = 64
NSA_GROUP = NSA_HEADS // NSA_KV_HEADS
CMP_LEN = 32
CMP_STRIDE = 16
CMP_HIDDEN = 128
SEL_BLOCK = 64
N_SEL = 16
WINDOW = 512
Q_BLOCK = 128
ML_HEADS = 4
ML_HEAD_DIM = 128
ML_CHUNK = 64
ML_CONV = 4
RW_HEADS = 8
RW_HEAD_DIM = 64
RW_DECAY_RANK = 64
RW_AICL_RANK = 64
RW_GATE_RANK = 128
D_FF = 5632
REL_BUCKETS = 32
REL_MAX_DIST = 128

NORM_EPS = 1e-6
RW_LN_EPS = 64e-5
MASK_NEG = -1e30

NSA_W = NSA_HEADS * NSA_HEAD_DIM
NSA_KV_W = NSA_KV_HEADS * NSA_HEAD_DIM
ML_W = ML_HEADS * ML_HEAD_DIM
RW_W = RW_HEADS * RW_HEAD_DIM
NSA_COLS = NSA_W + 6 * NSA_KV_W + 3 * NSA_HEADS
ML_COLS = 4 * ML_W + 2 * ML_HEADS
RW_COLS = 3 * RW_W + RW_DECAY_RANK + RW_AICL_RANK + RW_GATE_RANK
N_IN = NSA_COLS + ML_COLS + RW_COLS

kernel_name = "hybrid_nsa_mlstm_rwkv7_macaron"


def split_cols(z, sizes):
    return jnp.split(z, np.cumsum(sizes)[:-1].tolist(), axis=-1)


def rmsnorm(x, g):
    x32 = x.astype(jnp.float32)
    y = x32 * lax.rsqrt(jnp.mean(x32 * x32, axis=-1, keepdims=True) + NORM_EPS)
    return (y * g.astype(jnp.float32)).astype(x.dtype)


def head_norm(y, g, eps):
    mu = jnp.mean(y, axis=-1, keepdims=True)
    var = jnp.mean(jnp.square(y - mu), axis=-1, keepdims=True)
    return (y - mu) * lax.rsqrt(var + eps) * g.astype(jnp.float32).reshape(y.shape[-2:])


def swiglu(h, w_gate, w_up, w_down):
    return (jax.nn.silu(h @ w_gate) * (h @ w_up)) @ w_down


def rel_bucket(dist):
    n = jnp.maximum(dist, 0)
    max_exact = REL_BUCKETS // 2
    large = max_exact + (jnp.log(jnp.maximum(n, max_exact).astype(jnp.float32) / max_exact)
                         / math.log(REL_MAX_DIST / max_exact) * (REL_BUCKETS - max_exact)).astype(jnp.int32)
    return jnp.where(n < max_exact, n, jnp.minimum(large, REL_BUCKETS - 1))


def compress_kv(kv, pos, w1, w2):
    b_, t_, kvh, hd = kv.shape
    n_cmp = t_ // CMP_STRIDE
    kp = jnp.pad(kv, ((0, 0), (0, CMP_STRIDE), (0, 0), (0, 0))).reshape(b_, n_cmp + 1, CMP_STRIDE, kvh, hd)
    blocks = jnp.concatenate([kp[:, :-1], kp[:, 1:]], axis=2) + pos[None, None, :, None, :]
    flat = blocks.transpose(0, 3, 1, 2, 4).reshape(b_, kvh, n_cmp, CMP_LEN * hd)
    return jax.nn.silu(flat @ w1) @ w2


def nsa_attention(q, k_cmp, v_cmp, k_slc, v_slc, k_win, v_win, gates, rel_bias):
    b_, kvh, g_, t_, hd = q.shape
    n_cmp = k_cmp.shape[2]
    n_blk = k_slc.shape[2]
    n_sel = min(N_SEL, n_blk)
    tab = rel_bias.astype(jnp.float32)
    tab_g = tab.reshape(kvh, g_, REL_BUCKETS)
    cmp_end = jnp.arange(n_cmp, dtype=jnp.int32) * CMP_STRIDE + (CMP_LEN - 1)
    blk_ids = jnp.arange(n_blk, dtype=jnp.int32)
    bi = jnp.arange(b_)[:, None, None, None]
    hi = jnp.arange(kvh)[None, :, None, None]
    hi6 = jnp.arange(kvh)[None, :, None, None, None, None]
    gi6 = jnp.arange(g_)[None, None, :, None, None, None]

    def block(qb):
        s0 = qb * Q_BLOCK
        t = s0 + jnp.arange(Q_BLOCK, dtype=jnp.int32)
        qq = lax.dynamic_slice_in_dim(q, s0, Q_BLOCK, axis=3)
        d_cmp = t[:, None] - cmp_end[None, :]
        ok_cmp = d_cmp >= 0
        s = (jnp.einsum('bhgqd,bhnd->bhgqn', qq, k_cmp).astype(jnp.float32)
             + tab[:, rel_bucket(d_cmp)].reshape(kvh, g_, Q_BLOCK, n_cmp))
        p_cmp = jax.nn.softmax(jnp.where(ok_cmp, s, MASK_NEG), axis=-1) * ok_cmp.any(-1)[:, None]
        o_cmp = jnp.einsum('bhgqn,bhnd->bhgqd', p_cmp.astype(v_cmp.dtype), v_cmp)
        imp = p_cmp.sum(2).reshape(b_, kvh, Q_BLOCK, n_blk, SEL_BLOCK // CMP_STRIDE)
        score = imp.sum(-1) + jnp.pad(imp[..., :-1, -1], ((0, 0), (0, 0), (0, 0), (1, 0)))
        cur = t // SEL_BLOCK
        forced = (blk_ids[None, :] == 0) | (blk_ids[None, :] == cur[:, None]) | (blk_ids[None, :] == cur[:, None] - 1)
        ok_blk = blk_ids[None, :] * SEL_BLOCK <= t[:, None]
        score = jnp.where(forced, -MASK_NEG, jnp.where(ok_blk, score, MASK_NEG))
        _, idx = lax.top_k(score, n_sel)
        kg = k_slc[bi, hi, idx]
        vg = v_slc[bi, hi, idx]
        pos = idx[..., None] * SEL_BLOCK + jnp.arange(SEL_BLOCK, dtype=jnp.int32)
        d_slc = (t[:, None, None] - pos)[:, :, None]
        s = (jnp.einsum('bhgqd,bhqnld->bhgqnl', qq, kg).astype(jnp.float32)
             + tab_g[hi6, gi6, rel_bucket(d_slc)])
        s = jnp.where(d_slc >= 0, s, MASK_NEG)
        p = jax.nn.softmax(s.reshape(b_, kvh, g_, Q_BLOCK, -1), axis=-1).reshape(s.shape)
        o_slc = jnp.einsum('bhgqnl,bhqnld->bhgqd', p.astype(vg.dtype), vg)
        kw = lax.dynamic_slice_in_dim(k_win, s0, WINDOW + Q_BLOCK, axis=2)
        vw = lax.dynamic_slice_in_dim(v_win, s0, WINDOW + Q_BLOCK, axis=2)
        kpos = s0 - WINDOW + jnp.arange(WINDOW + Q_BLOCK, dtype=jnp.int32)
        d_win = t[:, None] - kpos[None, :]
        ok_win = (d_win >= 0) & (d_win < WINDOW) & (kpos >= 0)[None, :]
        s = (jnp.einsum('bhgqd,bhkd->bhgqk', qq, kw).astype(jnp.float32)
             + tab[:, rel_bucket(d_win)].reshape(kvh, g_, Q_BLOCK, -1))
        p = jax.nn.softmax(jnp.where(ok_win, s, MASK_NEG), axis=-1)
        o_win = jnp.einsum('bhgqk,bhkd->bhgqd', p.astype(vw.dtype), vw)
        gq = lax.dynamic_slice_in_dim(gates, s0, Q_BLOCK, axis=3).astype(qq.dtype)
        o = gq[..., 0:1] * o_cmp + gq[..., 1:2] * o_slc + gq[..., 2:3] * o_win
        return o.transpose(0, 3, 1, 2, 4).reshape(b_, Q_BLOCK, kvh * g_ * hd)

    out = lax.map(block, jnp.arange(t_ // Q_BLOCK, dtype=jnp.int32))
    return out.transpose(1, 0, 2, 3).reshape(b_, t_, kvh * g_ * hd)


def nsa_group(z, rel_bias, cmp_pos, cmp_w1, cmp_w2):
    b_, t_, _ = z.shape
    q, kc, vc, ks, vs, kw, vw, gt = split_cols(z, [NSA_W] + [NSA_KV_W] * 6 + [3 * NSA_HEADS])
    q = q.reshape(b_, t_, NSA_KV_HEADS, NSA_GROUP, NSA_HEAD_DIM).transpose(0, 2, 3, 1, 4) * NSA_HEAD_DIM ** -0.5
    kv4 = lambda a: a.reshape(b_, t_, NSA_KV_HEADS, NSA_HEAD_DIM)
    k_cmp = compress_kv(kv4(kc), cmp_pos[0], cmp_w1[0], cmp_w2[0])
    v_cmp = compress_kv(kv4(vc), cmp_pos[1], cmp_w1[1], cmp_w2[1])
    blocks = lambda a: kv4(a).transpose(0, 2, 1, 3).reshape(b_, NSA_KV_HEADS, t_ // SEL_BLOCK, SEL_BLOCK, NSA_HEAD_DIM)
    band = lambda a: jnp.pad(kv4(a).transpose(0, 2, 1, 3), ((0, 0), (0, 0), (WINDOW, 0), (0, 0)))
    gates = jax.nn.sigmoid(gt.astype(jnp.float32)).reshape(b_, t_, NSA_KV_HEADS, NSA_GROUP, 3).transpose(0, 2, 3, 1, 4)
    return nsa_attention(q, k_cmp, v_cmp, blocks(ks), blocks(vs), band(kw), band(vw), gates, rel_bias)


def causal_depthwise_conv(x, w, b):
    k_ = w.shape[0]
    t_ = x.shape[1]
    xp = jnp.pad(x, ((0, 0), (k_ - 1, 0), (0, 0)))
    return sum(xp[:, j:j + t_] * w[j] for j in range(k_)) + b


def mlstm_chunkwise(q, k, v, i_pre, f_pre):
    b_, t_, h_, d_ = q.shape
    L = ML_CHUNK
    nc = t_ // L
    ch4 = lambda a: a.reshape(b_, nc, L, h_, d_).transpose(1, 0, 3, 2, 4)
    ch3 = lambda a: a.reshape(b_, nc, L, h_).transpose(1, 0, 3, 2)
    qc, kc, vc = ch4(q), ch4(k * d_ ** -0.5), ch4(v)
    ic = ch3(i_pre)
    bc = jnp.cumsum(ch3(jax.nn.log_sigmoid(f_pre)), axis=-1)
    causal = jnp.tril(jnp.ones((L, L), dtype=bool))

    def step(carry, inp):
        C, n, m = carry
        qx, kx, vx, ix, bx = inp
        g = bx[..., -1]
        log_d = jnp.where(causal, bx[..., :, None] - bx[..., None, :] + ix[..., None, :], -jnp.inf)
        m_inter = bx + m[..., None]
        m_t = jnp.maximum(log_d.max(-1), m_inter)
        s = jnp.einsum('bhtd,bhsd->bhts', qx, kx) * jnp.exp(log_d - m_t[..., None])
        w_inter = jnp.exp(m_inter - m_t)
        num = jnp.einsum('bhts,bhsd->bhtd', s, vx) + w_inter[..., None] * jnp.einsum('bhtd,bhde->bhte', qx, C)
        den = s.sum(-1) + w_inter * jnp.einsum('bhtd,bhd->bht', qx, n)
        h = num / jnp.maximum(jnp.abs(den), jnp.exp(-m_t))[..., None]
        log_w = g[..., None] - bx + ix
        m_new = jnp.maximum(g + m, log_w.max(-1))
        wk = jnp.exp(log_w - m_new[..., None])
        dec = jnp.exp(g + m - m_new)
        C = dec[..., None, None] * C + jnp.einsum('bhs,bhsd,bhse->bhde', wk, kx, vx)
        n = dec[..., None] * n + jnp.einsum('bhs,bhsd->bhd', wk, kx)
        return (C, n, m_new), h

    init = (jnp.zeros((b_, h_, d_, d_), jnp.float32), jnp.zeros((b_, h_, d_), jnp.float32),
            jnp.zeros((b_, h_), jnp.float32))
    _, hs = lax.scan(step, init, (qc, kc, vc, ic, bc))
    return hs.transpose(1, 0, 3, 2, 4).reshape(b_, t_, h_, d_)


def mlstm_group(z, conv_w, conv_b, gate_b, norm_g):
    b_, t_, _ = z.shape
    qk, v, o, ig, fg = split_cols(z, [2 * ML_W, ML_W, ML_W, ML_HEADS, ML_HEADS])
    qk = jax.nn.silu(causal_depthwise_conv(qk, conv_w, conv_b)).astype(jnp.float32)
    q, k = jnp.split(qk, 2, axis=-1)
    hd = lambda a: a.astype(jnp.float32).reshape(b_, t_, ML_HEADS, ML_HEAD_DIM)
    i_pre = ig.astype(jnp.float32) + gate_b[0].astype(jnp.float32)
    f_pre = fg.astype(jnp.float32) + gate_b[1].astype(jnp.float32)
    h = mlstm_chunkwise(hd(q), hd(k), hd(v), i_pre, f_pre)
    h = jax.nn.sigmoid(hd(o)) * h
    return head_norm(h, norm_g, NORM_EPS).reshape(b_, t_, ML_W).astype(z.dtype)


def rwkv7_scan(r, w, k, v, kk, a):
    b_, t_, h_, d_ = r.shape
    seq = tuple(jnp.moveaxis(u, 1, 0) for u in (r, w, k, v, -kk, kk * a))

    def step(S, inp):
        rt, wt, kt, vt, at, bt = inp
        sa = jnp.einsum('bhij,bhj->bhi', S, at)
        S = S * wt[:, :, None, :] + sa[..., None] * bt[:, :, None, :] + vt[..., None] * kt[:, :, None, :]
        return S, jnp.einsum('bhij,bhj->bhi', S, rt)

    _, y = lax.scan(step, jnp.zeros((b_, h_, d_, d_), jnp.float32), seq)
    return jnp.moveaxis(y, 0, 1)


def rwkv_group(z, mu, w0, w_up, a0, a_up, g_up, k_k, k_a, r_k, ln):
    b_, t_, _ = z.shape
    z_prev = jnp.pad(z, ((0, 0), (1, 0), (0, 0)))[:, :-1]
    z = z + mu * (z_prev - z)
    r, k, v, wd, ad, gd = [u.astype(jnp.float32) for u in
                           split_cols(z, [RW_W, RW_W, RW_W, RW_DECAY_RANK, RW_AICL_RANK, RW_GATE_RANK])]
    f32 = lambda p: p.astype(jnp.float32)
    w_log = -jax.nn.softplus(-(f32(w0) + jnp.tanh(wd) @ f32(w_up))) - 0.5
    decay = jnp.exp(-jnp.exp(w_log))
    a = jax.nn.sigmoid(f32(a0) + ad @ f32(a_up))
    g = jax.nn.sigmoid(gd) @ f32(g_up)
    hd = lambda u: u.reshape(b_, t_, RW_HEADS, RW_HEAD_DIM)
    kk = hd(k * f32(k_k))
    kk = kk / jnp.maximum(jnp.linalg.norm(kk, axis=-1, keepdims=True), 1e-12)
    k = k * (1.0 + (a - 1.0) * f32(k_a))
    r4, k4, v4 = hd(r), hd(k), hd(v)
    y = rwkv7_scan(r4, hd(decay), k4, v4, kk, hd(a))
    y = head_norm(y, ln[0], RW_LN_EPS) + f32(ln[1]).reshape(RW_HEADS, RW_HEAD_DIM)
    y = y + jnp.sum(r4 * k4 * f32(r_k), axis=-1, keepdims=True) * v4
    return (y.reshape(b_, t_, RW_W) * g).astype(z.dtype)


def setup_inputs(seed: int = 0) -> dict:
    key = jax.random.key(seed)
    ks = iter(jax.random.split(key, 48))
    nrm = lambda shape, scale: jax.random.normal(next(ks), shape, jnp.float32) * scale
    hdim = NSA_HEAD_DIM
    return {
        "x": nrm((BATCH, SEQ, D_MODEL), 1.0),
        "c": nrm((BATCH, D_MODEL), 1.0),
        "rel_bias": nrm((NSA_HEADS, REL_BUCKETS), 0.5),
        "final_norm": 1.0 + nrm((D_MODEL,), 0.02),
        "ada_w": nrm((DEPTH, D_MODEL, 9 * D_MODEL), 0.5 * D_MODEL ** -0.5),
        "ada_b": nrm((DEPTH, 9 * D_MODEL), 0.02),
        "norm_g": 1.0 + nrm((DEPTH, 3, D_MODEL), 0.02),
        "ffn_w_gate": nrm((DEPTH, 2, D_MODEL, D_FF), D_MODEL ** -0.5),
        "ffn_w_up": nrm((DEPTH, 2, D_MODEL, D_FF), D_MODEL ** -0.5),
        "ffn_w_down": nrm((DEPTH, 2, D_FF, D_MODEL), D_FF ** -0.5),
        "w_in": nrm((DEPTH, D_MODEL, N_IN), D_MODEL ** -0.5),
        "w_out": nrm((DEPTH, D_MIX, D_MODEL), D_MIX ** -0.5),
        "cmp_pos": nrm((DEPTH, 2, CMP_LEN, hdim), 0.1),
        "cmp_w1": nrm((DEPTH, 2, CMP_LEN * hdim, CMP_HIDDEN), (CMP_LEN * hdim) ** -0.5),
        "cmp_w2": nrm((DEPTH, 2, CMP_HIDDEN, hdim), CMP_HIDDEN ** -0.5),
        "ml_conv_w": nrm((DEPTH, ML_CONV, 2 * ML_W), ML_CONV ** -0.5),
        "ml_conv_b": nrm((DEPTH, 2 * ML_W), 0.02),
        "ml_gate_b": jnp.stack([nrm((DEPTH, ML_HEADS), 0.1),
                                jnp.linspace(3.0, 6.0, ML_HEADS)[None, :] + nrm((DEPTH, ML_HEADS), 0.1)], axis=1),
        "ml_norm": 1.0 + nrm((DEPTH, ML_W), 0.02),
        "rw_mu": jax.random.uniform(next(ks), (DEPTH, RW_COLS), jnp.float32),
        "rw_w0": nrm((DEPTH, RW_W), 0.5),
        "rw_w_up": nrm((DEPTH, RW_DECAY_RANK, RW_W), 0.1),
        "rw_a0": nrm((DEPTH, RW_W), 0.5),
        "rw_a_up": nrm((DEPTH, RW_AICL_RANK, RW_W), 0.1),
        "rw_g_up": nrm((DEPTH, RW_GATE_RANK, RW_W), RW_GATE_RANK ** -0.5),
        "rw_k_k": 0.85 + nrm((DEPTH, RW_W), 0.05),
        "rw_k_a": 1.0 + nrm((DEPTH, RW_W), 0.05),
        "rw_r_k": nrm((DEPTH, RW_HEADS, RW_HEAD_DIM), 0.1),
        "rw_ln": jnp.stack([1.0 + nrm((DEPTH, RW_W), 0.02), nrm((DEPTH, RW_W), 0.02)], axis=1),
    }


def reference(x, c, rel_bias, final_norm, ada_w, ada_b, norm_g, ffn_w_gate, ffn_w_up, ffn_w_down,
              w_in, w_out, cmp_pos, cmp_w1, cmp_w2, ml_conv_w, ml_conv_b, ml_gate_b, ml_norm,
              rw_mu, rw_w0, rw_w_up, rw_a0, rw_a_up, rw_g_up, rw_k_k, rw_k_a, rw_r_k, rw_ln):
    b_ = x.shape[0]
    cond = jax.nn.silu(c)
    for l in range(DEPTH):
        mod = (cond @ ada_w[l] + ada_b[l]).reshape(b_, 3, 3, D_MODEL)

        def adaln(u, i):
            return rmsnorm(u, norm_g[l, i]) * (1.0 + mod[:, i, 1, None]) + mod[:, i, 0, None]

        h = adaln(x, 0)
        x = x + 0.5 * mod[:, 0, 2, None] * swiglu(h, ffn_w_gate[l, 0], ffn_w_up[l, 0], ffn_w_down[l, 0])
        h = adaln(x, 1)
        z = h @ w_in[l]
        z_nsa, z_ml, z_rw = jnp.split(z, [NSA_COLS, NSA_COLS + ML_COLS], axis=-1)
        y_nsa = nsa_group(z_nsa, rel_bias, cmp_pos[l], cmp_w1[l], cmp_w2[l])
        y_ml = mlstm_group(z_ml, ml_conv_w[l], ml_conv_b[l], ml_gate_b[l], ml_norm[l])
        y_rw = rwkv_group(z_rw, rw_mu[l], rw_w0[l], rw_w_up[l], rw_a0[l], rw_a_up[l], rw_g_up[l],
                          rw_k_k[l], rw_k_a[l], rw_r_k[l], rw_ln[l])
        y = jnp.concatenate([y_nsa, y_ml, y_rw], axis=-1) @ w_out[l]
        x = x + mod[:, 1, 2, None] * y
        h = adaln(x, 2)
        x = x + 0.5 * mod[:, 2, 2, None] * swiglu(h, ffn_w_gate[l, 1], ffn_w_up[l, 1], ffn_w_down[l, 1])
    return rmsnorm(x, final_norm)
```

```python
import contextlib
import numpy as np
import concourse.bass as bass
import concourse.mybir as mybir
from concourse.alu_op_type import AluOpType as ALU

F32 = mybir.dt.float32
BF16 = mybir.dt.bfloat16
AF = mybir.ActivationFunctionType
AX = mybir.AxisListType


class Trk:
    __slots__ = ("w", "r", "name")

    def __init__(self, name=""):
        self.w = {}
        self.r = {}
        self.name = name


class V:
    __slots__ = ("ap", "t")

    def __init__(self, ap, t):
        self.ap = ap
        self.t = t

    def __getitem__(self, idx):
        return V(self.ap[idx], self.t)

    def re(self, pat, **kw):
        return V(self.ap.rearrange(pat, **kw), self.t)


class Tile:
    def __init__(self, kb, handle, name):
        self.h = handle
        self.t = Trk(name)
        self.dsem = None
        self.dcnt = 0
        self.name = name
        self.subs = {}

    def __getitem__(self, idx):
        return V(self.h[idx], self.t)

    def sub(self, key):
        if key not in self.subs:
            s = Tile.__new__(Tile)
            s.h = self.h; s.t = Trk(f"{self.name}.{key}"); s.dsem = None; s.dcnt = 0
            s.name = f"{self.name}.{key}"; s.subs = {}
            self.subs[key] = s
        return self.subs[key]


class Eng:
    def __init__(self, name, obj, sem):
        self.name = name
        self.o = obj
        self.sem = sem
        self.cnt = 0
        self.known = {}


class KB:
    def __init__(self):
        self.nc = bass.Bass("TRN2", target_bir_lowering=False)
        self.es = contextlib.ExitStack()
        nc = self.nc
        self.eng = {}
        for nm, o in (("pe", nc.tensor), ("act", nc.scalar), ("dve", nc.vector), ("pool", nc.gpsimd), ("sp", nc.sync)):
            sem = self.es.enter_context(nc.semaphore(f"prog_{nm}"))
            self.eng[nm] = Eng(nm, o, sem)
        self.free_dsems = []
        self.n_dsem = 0
        self.dtiles = []
        self.bar_sem = self.es.enter_context(nc.semaphore("barrier"))
        self.bar_cnt = 0
        self.ninst = 0

    def din(self, name, shape, dt=F32):
        return self.nc.dram_tensor(name, list(shape), dt, kind="ExternalInput").ap()

    def dout(self, name, shape, dt=F32):
        return self.nc.dram_tensor(name, list(shape), dt, kind="ExternalOutput").ap()

    def dscr(self, name, shape, dt=F32):
        return self.nc.dram_tensor(name, list(shape), dt, kind="Internal").ap()

    def sb(self, name, shape, dt=F32, stack=None):
        h = (stack or self.es).enter_context(self.nc.sbuf_tensor(name, list(shape), dt))
        return Tile(self, h, name)

    def ps(self, name, shape, dt=F32, stack=None):
        h = (stack or self.es).enter_context(self.nc.psum_tensor(name, list(shape), dt))
        return Tile(self, h, name)

    def _need(self, deps, ev):
        for sid, (s, v) in ev.items():
            if sid not in deps or deps[sid][1] < v:
                deps[sid] = (s, v)

    def _wait(self, e, deps):
        for sid, (s, v) in deps.items():
            if e.known.get(sid, 0) < v:
                e.o.wait_ge(s, v)
                e.known[sid] = v

    def _deps(self, e, reads, writes, same_eng_raw=True):
        deps = {}
        for t in reads:
            self._need(deps, t.w)
        for t in writes:
            self._need(deps, t.w)
            self._need(deps, t.r)
        own = id(e.sem)
        if own in deps:
            if e.name == "pe":
                del deps[own]
        return deps

    def op(self, en, fn, reads, writes):
        e = self.eng[en]
        rt = [v.t for v in reads]
        wt = [v.t for v in writes]
        self._wait(e, self._deps(e, rt, wt))
        ins = fn(e.o)
        e.cnt += 1
        ins.then_inc(e.sem, 1)
        ev = (e.sem, e.cnt)
        sid = id(e.sem)
        for t in wt:
            t.w = {sid: ev}
            t.r = {}
        for t in rt:
            if t not in wt:
                t.r[sid] = ev
        self.ninst += 1
        return ins

    def _dsem(self, tile_like, q="sp"):
        if tile_like.dsem is None:
            tile_like.sw = (q == "pool")
            if self.free_dsems and not tile_like.sw:
                tile_like.dsem, tile_like.dcnt = self.free_dsems.pop()
            else:
                tile_like.dsem = self.es.enter_context(self.nc.semaphore(f"d{self.n_dsem}"))
                tile_like.dcnt = 0
                self.n_dsem += 1
                assert self.n_dsem < 120, "too many dma semaphores"
            self.dtiles.append(tile_like)
        return tile_like.dsem

    def dma(self, q, out, in_, sem_tile, reads=(), writes=(), **kw):
        e = self.eng[q]
        rt = [in_.t] if isinstance(in_, V) else []
        wt = [out.t] if isinstance(out, V) else []
        rt += list(reads)
        wt += list(writes)
        self._wait(e, self._deps(e, rt, wt, same_eng_raw=True))
        oap = out.ap if isinstance(out, V) else out
        iap = in_.ap if isinstance(in_, V) else in_
        sem = self._dsem(sem_tile, q)
        ins = e.o.dma_start(out=oap, in_=iap, **kw)
        sem_tile.dcnt += 16
        ins.then_inc(sem, 16)
        ev = (sem, sem_tile.dcnt)
        sid = id(sem)
        for t in wt:
            t.w = {sid: ev}
            t.r = {}
        for t in rt:
            if t not in wt:
                t.r[sid] = ev
        self.ninst += 1
        return ins

    def barrier(self):
        sp = self.eng["sp"]
        deps = {}
        for nm, e in self.eng.items():
            if e.cnt:
                deps[id(e.sem)] = (e.sem, e.cnt)
        for tl in self.dtiles:
            if tl.dcnt:
                deps[id(tl.dsem)] = (tl.dsem, tl.dcnt)
        self._wait(sp, deps)
        self.bar_cnt += 1
        sp.o.sem_inc(self.bar_sem, 1) if False else None
        ins = sp.o.nop()
        sp.cnt += 1
        ins.then_inc(sp.sem, 1)
        tgt = sp.cnt
        for nm, e in self.eng.items():
            if nm == "sp":
                continue
            e.o.wait_ge(sp.sem, tgt)
            e.known[id(sp.sem)] = tgt
            for sid, sv in deps.items():
                e.known[sid] = max(e.known.get(sid, 0), sv[1])
        keep = []
        for tl in self.dtiles:
            if getattr(tl, "sw", False):
                keep.append(tl)
            else:
                self.free_dsems.append((tl.dsem, tl.dcnt))
                tl.dsem = None
        self.dtiles = keep

    def finish(self):
        self.barrier()

    def mm(self, out, lhsT, rhs, start=True, stop=True, **kw):
        return self.op("pe", lambda o: o.matmul(out.ap, lhsT.ap, rhs.ap, start=start, stop=stop, **kw),
                       [lhsT, rhs] + ([] if start else [out]), [out])

    def tr(self, out, in_, ident):
        return self.op("pe", lambda o: o.transpose(out.ap, in_.ap, ident.ap), [in_, ident], [out])

    def act(self, out, in_, func, bias=None, scale=None, accum_out=None, en="act"):
        reads = [in_]
        kw = {}
        if bias is not None:
            if isinstance(bias, V):
                reads.append(bias); kw["bias"] = bias.ap
            else:
                kw["bias"] = bias
        if scale is not None:
            if isinstance(scale, V):
                reads.append(scale); kw["scale"] = scale.ap
            else:
                kw["scale"] = scale
        writes = [out]
        if accum_out is not None:
            writes.append(accum_out); kw["accum_out"] = accum_out.ap
        return self.op("act", lambda o: o.activation(out.ap, in_.ap, func, **kw), reads, writes)

    def tt(self, en, out, in0, in1, op):
        return self.op(en, lambda o: o.tensor_tensor(out.ap, in0.ap, in1.ap, op), [in0, in1], [out])

    def ts(self, en, out, in0, s1, s2=None, op0=ALU.mult, op1=None, accum_out=None):
        reads = [in0]
        a1 = s1
        if isinstance(s1, V):
            reads.append(s1); a1 = s1.ap
        a2 = s2
        if isinstance(s2, V):
            reads.append(s2); a2 = s2.ap
        kw = {}
        writes = [out]
        if op1 is not None:
            kw["op1"] = op1
        if accum_out is not None:
            kw["accum_out"] = accum_out.ap; writes.append(accum_out)
        return self.op(en, lambda o: o.tensor_scalar(out.ap, in0.ap, a1, a2, op0, **kw), reads, writes)

    def stt(self, out, in0, scalar, in1, op0, op1, en="dve"):
        reads = [in0, in1]
        a = scalar
        if isinstance(scalar, V):
            reads.append(scalar); a = scalar.ap
        return self.op(en, lambda o: o.scalar_tensor_tensor(out.ap, in0.ap, a, in1.ap, op0, op1), reads, [out])

    def copy(self, en, out, in_):
        if en == "act":
            return self.op("act", lambda o: o.copy(out.ap, in_.ap), [in_], [out])
        return self.op(en, lambda o: o.tensor_copy(out.ap, in_.ap), [in_], [out])

    def memset(self, en, out, val):
        return self.op(en, lambda o: o.memset(out.ap, val), [], [out])

    def scan(self, out, d0, d1, initial, op0, op1):
        reads = [d0, d1]
        a = initial
        if isinstance(initial, V):
            reads.append(initial); a = initial.ap
        return self.op("dve", lambda o: o.tensor_tensor_scan(out.ap, d0.ap, d1.ap, a, op0, op1), reads, [out])

    def reduce(self, out, in_, op, axis=AX.X, en="dve", **kw):
        return self.op(en, lambda o: o.tensor_reduce(out.ap, in_.ap, axis, op, **kw), [in_], [out])

    def max8(self, out, in_):
        return self.op("dve", lambda o: o.max(out.ap, in_.ap), [in_], [out])

    def match_replace(self, out, in_to_replace, in_values, imm):
        return self.op("dve", lambda o: o.match_replace(out.ap, in_to_replace.ap, in_values.ap, imm),
                       [in_to_replace, in_values], [out])

    def recip(self, out, in_):
        return self.op("dve", lambda o: o.reciprocal(out.ap, in_.ap), [in_], [out])


def carve(tile, rows, c0, n, dt):
    nf = n if dt == F32 else n // 2
    ap = tile.h[rows, c0:c0 + nf]
    if dt != F32:
        ap = ap.bitcast(dt)
    return V(ap, tile.t)

import contextlib

D = 2048
DC = 16
DFF = 5632
FC = 44
TT = 512
EPS = 1e-6


class RL:
    def __init__(self, k, NT):
        self.k = k
        self.NT = NT
        self.psb = [k.ps(f"psb{i}", [128, TT], F32) for i in range(8)]
        self.wi = 0
        self.di = 0
        self.pi = 0
        self.ti = 0

    def alloc(self):
        k = self.k
        self.xt = k.sb("xt", [128, DC, TT], F32)
        self.ht = k.sb("ht", [128, DC, TT], BF16)
        self.actT = k.sb("actT", [128, FC, TT], BF16)
        self.wgu = [[k.sb(f"wgu{i}{j}", [128, DC, 256], BF16) for j in range(2)] for i in range(2)]
        self.wd = [k.sb(f"wdt{i}", [128, FC, 256], BF16) for i in range(2)]
        self.tmp = [k.sb(f"tmp{i}", [128, TT], F32) for i in range(2)]
        self.sq = [k.sb(f"sq{i}", [128, TT], BF16) for i in range(2)]
        self.rstd = k.sb("rstd", [128, TT], F32)
        self.ones = k.sb("ones", [128, 128], BF16)
        k.memset("dve", self.ones[:], 1.0)
        self.epsc = k.sb("epsc", [128, 1], F32)
        k.memset("dve", self.epsc[:], EPS)

    def nps(self):
        p = self.psb[self.pi % 8]
        self.pi += 1
        return p

    def mod(self, name, condT, ada_w, ada_b_t, blks):
        k = self.k
        out = k.sb(name, [128, len(blks), 16], F32)
        with contextlib.ExitStack() as st:
            wt = [k.sb(f"{name}_w{i}", [128, DC, 512], F32, stack=st) for i in range(2)]
            awr = ada_w.rearrange("(kc p) j -> p kc j", p=128)
            n = 0
            for bi, blk in enumerate(blks):
                ps = self.nps()
                for cg in range(4):
                    w = wt[n % 2]; n += 1
                    j0 = blk * D + cg * 512
                    k.dma("sp", w[:], awr[:, :, j0:j0 + 512], w)
                    for cc in range(4):
                        for kc in range(DC):
                            k.mm(ps[:, cg * 4 + cc: cg * 4 + cc + 1], w[:, kc, cc * 128:(cc + 1) * 128],
                                 condT[:, kc:kc + 1], start=(kc == 0), stop=(kc == DC - 1))
                k.tt("dve", out[:, bi, :], ps[:, 0:16], ada_b_t[:, blk * 16:(blk + 1) * 16], ALU.add)
        return out

    def load_x(self, xT_d, t0):
        k = self.k
        k.dma("sp", self.xt[:], xT_d.rearrange("c p t -> p c t")[:, :, t0:t0 + TT], self.xt)

    def store_x(self, xT_d, t0):
        k = self.k
        k.dma("sp", xT_d.rearrange("c p t -> p c t")[:, :, t0:t0 + TT], self.xt[:], self.xt)

    def rstd_calc(self):
        k = self.k
        ps = self.nps()
        for c in range(DC):
            s = self.sq[c % 2]
            k.act(s[:], self.xt[:, c, :], AF.Square)
            k.mm(ps[:], self.ones[:], s[:], start=(c == 0), stop=(c == DC - 1))
        k.act(self.rstd[:], ps[:], AF.Sqrt, bias=self.epsc[:], scale=1.0 / D)
        k.recip(self.rstd[:], self.rstd[:])

    def adaln(self, G, Sh, out=None):
        k = self.k
        out = out or self.ht
        self.rstd_calc()
        for c in range(DC):
            t = self.tmp[c % 2]
            k.tt("pool" if c % 2 else "dve", t[:], self.xt[:, c, :], self.rstd[:], ALU.mult)
            if Sh is not None:
                k.ts("dve", out[:, c, :], t[:], G[:, c:c + 1], Sh[:, c:c + 1], ALU.mult, ALU.add)
            else:
                k.ts("dve", out[:, c, :], t[:], G[:, c:c + 1], None, ALU.mult)

    def ffn(self, wg_d, wu_d, wd_d, gcol):
        k = self.k
        wgr = wg_d.rearrange("(kc p) f -> p kc f", p=128)
        wur = wu_d.rearrange("(kc p) f -> p kc f", p=128)
        wdr = wd_d.rearrange("(fc p) d -> p fc d", p=128)
        for fp in range(FC // 2):
            wg, wu = self.wgu[self.wi % 2]; self.wi += 1
            k.dma("pool", wg[:], wgr[:, :, fp * 256:(fp + 1) * 256], wg)
            k.dma("pool", wu[:], wur[:, :, fp * 256:(fp + 1) * 256], wu)
            for j in range(2):
                fc = fp * 2 + j
                pg = self.nps(); pu = self.nps()
                for kc in range(DC):
                    k.mm(pg[:], wg[:, kc, j * 128:(j + 1) * 128], self.ht[:, kc, :], start=(kc == 0), stop=(kc == DC - 1))
                for kc in range(DC):
                    k.mm(pu[:], wu[:, kc, j * 128:(j + 1) * 128], self.ht[:, kc, :], start=(kc == 0), stop=(kc == DC - 1))
                t = self.tmp[self.ti % 2]; self.ti += 1
                k.act(t[:], pg[:], AF.Silu)
                k.tt("dve", self.actT[:, fc, :], t[:], pu[:], ALU.mult)
        for dp in range(DC // 2):
            wd = self.wd[self.di % 2]; self.di += 1
            k.dma("pool", wd[:], wdr[:, :, dp * 256:(dp + 1) * 256], wd)
            for j in range(2):
                dc = dp * 2 + j
                ps = self.nps()
                for fc in range(FC):
                    k.mm(ps[:], wd[:, fc, j * 128:(j + 1) * 128], self.actT[:, fc, :], start=(fc == 0), stop=(fc == FC - 1))
                k.stt(self.xt[:, dc, :], ps[:], gcol[:, dc:dc + 1], self.xt[:, dc, :], ALU.mult, ALU.add)

    def mix(self, yT_d, t0, wo_d, gcol):
        k = self.k
        k.dma("sp", self.ht[:], yT_d.rearrange("c p t -> p c t")[:, :, t0:t0 + TT], self.ht)
        wor = wo_d.rearrange("(kc p) d -> p kc d", p=128)
        for dp in range(DC // 2):
            w = self.wgu[self.wi % 2][0]; self.wi += 1
            k.dma("pool", w[:], wor[:, :, dp * 256:(dp + 1) * 256], w)
            for j in range(2):
                dc = dp * 2 + j
                ps = self.nps()
                for kc in range(DC):
                    k.mm(ps[:], w[:, kc, j * 128:(j + 1) * 128], self.ht[:, kc, :], start=(kc == 0), stop=(kc == DC - 1))
                k.stt(self.xt[:, dc, :], ps[:], gcol[:, dc:dc + 1], self.xt[:, dc, :], ALU.mult, ALU.add)

import contextlib
import numpy as np

D = 2048; DC = 16
NSA_COLS = 2608; ML_BASE = 2608; RW_BASE = 4664
def core_cols(g):
    r = np.arange
    fm = [
        g * 256 + r(128), g * 256 + 128 + r(128),
        np.concatenate([1024 + g * 64 + r(64), 1280 + g * 64 + r(64)]),
        np.concatenate([1536 + g * 64 + r(64), 2048 + g * 64 + r(64)]),
        ML_BASE + g * 128 + r(128), ML_BASE + 512 + g * 128 + r(128),
        RW_BASE + g * 128 + r(128), RW_BASE + 512 + g * 128 + r(128), RW_BASE + 1024 + g * 128 + r(128),
        RW_BASE + 1536 + r(128),
        RW_BASE + 1664 + r(128),
        np.array([ML_BASE + 2048 + g, ML_BASE + 2052 + g]),
    ]
    tm = [1792 + g * 64 + r(64), 2304 + g * 64 + r(64),
          ML_BASE + 1024 + g * 128 + r(128), ML_BASE + 1536 + g * 128 + r(128),
          2560 + g * 12 + r(12)]
    return np.concatenate(fm + tm)
NFM = 11 * 128 + 2
NCOL = NFM + 396


class Scr:
    def __init__(self, k, T, pfx=""):
        self.T = T
        self.q = k.dscr(pfx + "s_q", [256, T], BF16)
        self.kcvc = k.dscr(pfx + "s_kcvc", [128, T + 16], BF16)
        self.kskw = k.dscr(pfx + "s_kskw", [128, T], BF16)
        self.mlq = k.dscr(pfx + "s_mlq", [128, T + 3], F32)
        self.mlk = k.dscr(pfx + "s_mlk", [128, T + 3], F32)
        self.rw = [k.dscr(pfx + f"s_rw{i}", [128, T + 1], F32) for i in range(5)]
        self.mlif = k.dscr(pfx + "s_mlif", [2, T], F32)
        self.tmb = k.dscr(pfx + "s_tmb", [T, 384], BF16)
        self.tmg = k.dscr(pfx + "s_tmg", [T, 12], F32)


def emit_P(k, T, hT_d, w_d, scr, psb):
    TT = 512
    with contextlib.ExitStack() as st:
        W = k.sb("P_W", [128, DC, NCOL], BF16, stack=st)
        wr = w_d.rearrange("(kc p) n -> p kc n", p=128)
        for kc in range(DC):
            k.dma("pool", W[:, kc, :], wr[:, kc, :], W)
        hts = [k.sb(f"P_h{i}", [128, DC, TT], BF16, stack=st) for i in range(2)]
        stg = [k.sb(f"P_s{i}", [128, TT], F32, stack=st) for i in range(4)]
        stb = [k.sb(f"P_b{i}", [128, TT], BF16, stack=st) for i in range(4)]
        stt_ = [k.sb(f"P_t{i}", [128, 396], BF16, stack=st) for i in range(2)]
        stg2 = [k.sb(f"P_g{i}", [128, 12], F32, stack=st) for i in range(2)]
        z = k.sb("P_z", [128, 16], F32, stack=st)
        k.memset("dve", z[:], 0.0)
        zb = k.sb("P_zb", [128, 16], BF16, stack=st)
        k.memset("dve", zb[:], 0.0)
        k.dma("sp", scr.kcvc[:, T:T + 16], zb[:], zb)
        k.dma("sp", scr.mlq[:, 0:3], z[:, 0:3], z)
        k.dma("sp", scr.mlk[:, 0:3], z[:, 0:3], z)
        for i in range(5):
            k.dma("sp", scr.rw[i][:, 0:1], z[:, 0:1], z, allow_slow_non_contiguous=True)
        hr = hT_d.rearrange("c p t -> p c t")
        pi = 0; si = 0; bi = 0
        for ti in range(T // TT):
            t0 = ti * TT
            h = hts[ti % 2]
            k.dma("sp", h[:], hr[:, :, t0:t0 + TT], h)
            for g in range(12):
                m = 128 if g < 11 else 2
                ps = psb[pi % 7]; pi += 1
                for kc in range(DC):
                    k.mm(ps[0:m, :], W[:, kc, g * 128:g * 128 + m], h[:, kc, :], start=(kc == 0), stop=(kc == DC - 1))
                if g < 4:
                    s = stb[bi % 4]; bi += 1
                    if g < 2:
                        k.act(s[:], ps[:], AF.Copy, scale=0.125)
                        k.dma("sp", scr.q[g * 128:(g + 1) * 128, t0:t0 + TT], s[:], s)
                    else:
                        k.copy("dve", s[:], ps[:])
                        dst = scr.kcvc if g == 2 else scr.kskw
                        k.dma("sp", dst[:, t0:t0 + TT], s[:], s)
                else:
                    s = stg[si % 4]; si += 1
                    if g % 2:
                        k.act(s[0:m, :], ps[0:m, :], AF.Copy)
                    else:
                        k.copy("dve", s[0:m, :], ps[0:m, :])
                    if g == 4:
                        k.dma("sp", scr.mlq[:, 3 + t0:3 + t0 + TT], s[:], s)
                    elif g == 5:
                        k.dma("sp", scr.mlk[:, 3 + t0:3 + t0 + TT], s[:], s)
                    elif g < 11:
                        k.dma("sp", scr.rw[g - 6][:, 1 + t0:1 + t0 + TT], s[:], s)
                    else:
                        k.dma("sp", scr.mlif[:, t0:t0 + TT], s[0:2, :], s)
            for j in range(4):
                ps = psb[pi % 7]; pi += 1
                for kc in range(DC):
                    k.mm(ps[:, 0:396], h[:, kc, j * 128:(j + 1) * 128], W[:, kc, NFM:NFM + 396], start=(kc == 0), stop=(kc == DC - 1))
                s = stt_[j % 2]; s2 = stg2[j % 2]
                k.act(s[:, 0:384], ps[:, 0:384], AF.Copy)
                k.copy("dve", s2[:], ps[:, 384:396])
                k.dma("sp", scr.tmb[t0 + j * 128:t0 + (j + 1) * 128, :], s[:, 0:384], s)
                k.dma("sp", scr.tmg[t0 + j * 128:t0 + (j + 1) * 128, :], s2[:], s2)
    k.barrier()

import contextlib

C_ID, C_SU, C_LE, C_LT = 0, 128, 256, 384


def emit_ML(k, T, scr, cst, mlp_d, mlg_d, yT_d, psb):
    NCH = T // 128
    with contextlib.ExitStack() as st:
        sb = lambda n, s, d=F32: k.sb("ML_" + n, s, d, stack=st)
        mlp = sb("mlp", [128, 12]); k.dma("sp", mlp[:], mlp_d, mlp)
        gB = sb("gB", [128, 128]); k.dma("sp", gB[:], mlg_d, gB)
        one1 = sb("one1", [128, 1]); k.memset("dve", one1[:], 1.0)
        eps1 = sb("eps1", [128, 1]); k.memset("dve", eps1[:], 1e-6)
        identb = sb("identb", [128, 128], BF16); k.copy("dve", identb[:], cst[:, C_ID:C_ID + 128])
        maskb = sb("maskb", [128, 128], BF16); k.copy("dve", maskb[:], cst[:, C_LE:C_LE + 128])
        ones = sb("ones", [128, 128]); k.memset("dve", ones[:], 1.0)
        zeros = sb("zeros", [128, 128]); k.memset("dve", zeros[:], 0.0)
        ig = sb("ig", [128, 128]); fg = sb("fg", [128, 128])
        ifr = scr.mlif.rearrange("r (c j) -> r c j", j=128)
        k.dma("sp", ig[0:NCH, :], ifr[0], ig)
        k.dma("sp", fg[0:NCH, :], ifr[1], fg)
        N = slice(0, NCH)
        k.ts("dve", ig[N, :], ig[N, :], mlp[N, 10:11], None, ALU.add)
        t1 = sb("t1", [128, 128]); t2 = sb("t2", [128, 128])
        k.ts("dve", fg[N, :], fg[N, :], mlp[N, 11:12], None, ALU.add)
        k.act(t1[N, :], fg[N, :], AF.Exp, scale=-1.0)
        k.act(t1[N, :], t1[N, :], AF.Ln, bias=one1[N, :])
        k.ts("dve", t1[N, :], t1[N, :], -1.0, None, ALU.mult)
        Bg = sb("Bg", [128, 128])
        k.scan(Bg[N, :], ones[N, :], t1[N, :], 0.0, ALU.mult, ALU.add)
        ps = psb[0]
        k.mm(ps[N, 0:1], cst[N, C_SU:C_SU + NCH], Bg[N, 127:128])
        pre = sb("pre", [128, 1]); k.copy("dve", pre[N, :], ps[N, 0:1])
        k.ts("dve", Bg[N, :], Bg[N, :], pre[N, :], None, ALU.add)
        u = sb("u", [128, 128]); k.tt("dve", u[N, :], ig[N, :], Bg[N, :], ALU.subtract)
        Uw = sb("Uw", [128, 128])
        k.scan(Uw[N, :], u[N, :], u[N, :], -1e30, ALU.max, ALU.max)
        ps = psb[1]
        k.mm(ps[0:1, 0:NCH], Uw[N, 127:128], cst[N, C_ID:C_ID + NCH])
        row = sb("row", [1, 128]); k.copy("dve", row[:, 0:NCH], ps[0:1, 0:NCH])
        inc = sb("inc", [1, 129]); k.memset("dve", inc[:], 0.0)
        k.scan(inc[:, 1:NCH + 1], row[:, 0:NCH], row[:, 0:NCH], 0.0, ALU.max, ALU.max)
        decr = sb("decr", [1, 128])
        k.tt("dve", decr[:, 0:NCH], inc[:, 0:NCH], inc[:, 1:NCH + 1], ALU.subtract)
        k.act(decr[:, 0:NCH], decr[:, 0:NCH], AF.Exp)
        ps = psb[2]
        k.mm(ps[:, 0:NCH], ones[0:1, :], decr[:, 0:NCH])
        decB = sb("decB", [128, 128]); k.copy("dve", decB[:, 0:NCH], ps[:, 0:NCH])
        ps = psb[3]
        k.mm(ps[N, 0:1], inc[:, 0:NCH], one1[0:1, :])
        k.mm(ps[N, 1:2], inc[:, 1:NCH + 1], one1[0:1, :])
        McMn = sb("McMn", [128, 2]); k.copy("dve", McMn[N, :], ps[N, 0:2])
        Ut = sb("Ut", [128, 128]); k.ts("dve", Ut[N, :], Uw[N, :], McMn[N, 0:1], None, ALU.max)
        tab = sb("tab", [128, 4, 128])
        k.ts("dve", t2[N, :], u[N, :], McMn[N, 0:1], None, ALU.subtract); k.act(tab[N, 0, :], t2[N, :], AF.Exp)
        k.ts("dve", t2[N, :], Ut[N, :], McMn[N, 0:1], None, ALU.subtract); k.act(tab[N, 1, :], t2[N, :], AF.Exp, scale=-1.0)
        k.tt("dve", t2[N, :], Bg[N, :], Ut[N, :], ALU.add); k.act(tab[N, 2, :], t2[N, :], AF.Exp, scale=-1.0)
        k.ts("dve", t2[N, :], u[N, :], McMn[N, 1:2], None, ALU.subtract); k.act(tab[N, 3, :], t2[N, :], AF.Exp)
        tabT = sb("tabT", [128, 4, 128])
        for i in range(4):
            ps = psb[3 + i]
            k.mm(ps[:, 0:NCH], tab[N, i, :], cst[N, C_ID:C_ID + NCH])
            k.copy("dve", tabT[:, i, 0:NCH], ps[:, 0:NCH])
        qT = sb("qT", [128, T], BF16); kT = sb("kT", [128, T], BF16)
        ktok = sb("ktok", [128, NCH, 128], BF16)
        vaug = sb("vaug", [128, NCH, 130], BF16)
        k.memset("dve", vaug[:, :, 128:130], 1.0)
        k.dma("sp", vaug[:, :, 0:128], scr.tmb.rearrange("(c s) n -> s c n", s=128)[:, :, 128:256], vaug)
        CT = 1024
        cin = [sb(f"cin{i}", [128, CT + 3]) for i in range(2)]
        cacc = [sb(f"cacc{i}", [128, CT]) for i in range(2)]
        n = 0
        for which, (src, dst, wc) in enumerate(((scr.mlq, qT, 0), (scr.mlk, kT, 5))):
            for t0 in range(0, T, CT):
                ci = cin[n % 2]; ca = cacc[n % 2]; n += 1
                k.dma("sp", ci[:], src[:, t0:t0 + CT + 3], ci)
                k.ts("dve", ca[:], ci[:, 0:CT], mlp[:, wc:wc + 1], mlp[:, wc + 4:wc + 5], ALU.mult, ALU.add)
                for j in range(1, 4):
                    k.stt(ca[:], ci[:, j:j + CT], mlp[:, wc + j:wc + j + 1], ca[:], ALU.mult, ALU.add)
                if which == 0:
                    k.act(dst[:, t0:t0 + CT], ca[:], AF.Silu)
                else:
                    k.act(ca[:], ca[:], AF.Silu)
                    k.ts("pool", dst[:, t0:t0 + CT], ca[:], 128 ** -0.5, None, ALU.mult)
        ptb = [carve(psb[7], slice(0, 128), i * 64, 128, BF16) for i in range(2)]
        for c in range(NCH):
            p = ptb[c % 2]
            k.tr(p[:], kT[:, c * 128:(c + 1) * 128], identb[:])
            k.copy("act" if c % 2 else "dve", ktok[:, c, :], p[:])
        Cst = sb("Cst", [128, 130]); k.memset("dve", Cst[:], 0.0)
        Cbf = sb("Cbf", [128, 130], BF16); k.memset("dve", Cbf[:], 0.0)
        STs = [sb(f"STs{i}", [128, 128], BF16) for i in range(2)]
        kws = [sb(f"kws{i}", [128, 128], BF16) for i in range(2)]
        numw = [sb(f"numw{i}", [128, 130]) for i in range(2)]
        den = [sb(f"den{i}", [128, 1]) for i in range(2)]
        ot = [sb(f"ot{i}", [128, 128], BF16) for i in range(2)]
        hh = [sb(f"hh{i}", [128, 128]) for i in range(2)]
        stats = [sb(f"stats{i}", [128, 6]) for i in range(2)]
        mv = [sb(f"mv{i}", [128, 2]) for i in range(2)]
        yb = [sb(f"yb{i}", [128, 128], BF16) for i in range(2)]
        yo = [sb(f"yo{i}", [128, 128], BF16) for i in range(2)]
        otr = scr.tmb.rearrange("(c s) n -> c s n", s=128)
        for c in range(NCH):
            b = c % 2
            cs = slice(c * 128, (c + 1) * 128)
            k.dma("sp", ot[b][:], otr[c][:, 256:384], ot[b])
            pS = psb[(3 * c) % 6]; pN = psb[(3 * c + 1) % 6]; pC = psb[(3 * c + 2) % 6]
            k.mm(pS[:, 0:128], kT[:, cs], qT[:, cs])
            k.stt(STs[b][:], pS[:, 0:128], tabT[:, 0, c:c + 1], maskb[:], ALU.mult, ALU.mult)
            k.mm(pN[:, 0:130], STs[b][:], vaug[:, c, :], start=True, stop=False)
            k.mm(pN[:, 0:130], qT[:, cs], Cbf[:], start=False, stop=True)
            k.ts("pool", kws[b][:], ktok[:, c, :], tabT[:, 3, c:c + 1], None, ALU.mult)
            k.mm(pC[:, 0:130], kws[b][:], vaug[:, c, :])
            k.stt(Cst[:], Cst[:], decB[:, c:c + 1], pC[:, 0:130], ALU.mult, ALU.add)
            k.copy("act", Cbf[:], Cst[:])
            k.ts("dve", numw[b][:], pN[:, 0:130], tabT[:, 1, c:c + 1], None, ALU.mult)
            k.act(den[b][:], numw[b][:, 128:129], AF.Abs)
            k.ts("dve", den[b][:], den[b][:], tabT[:, 2, c:c + 1], None, ALU.max)
            k.recip(den[b][:], den[b][:])
            k.act(ot[b][:], ot[b][:], AF.Sigmoid)
            k.stt(hh[b][:], numw[b][:, 0:128], den[b][:], ot[b][:], ALU.mult, ALU.mult)
            k.op("dve", lambda o, b=b: o.bn_stats(stats[b].h[:], hh[b].h[:]), [hh[b][:]], [stats[b][:]])
            k.op("dve", lambda o, b=b: o.bn_aggr(mv[b].h[:], stats[b].h[:]), [stats[b][:]], [mv[b][:]])
            k.act(mv[b][:, 1:2], mv[b][:, 1:2], AF.Sqrt, bias=eps1[:])
            k.recip(mv[b][:, 1:2], mv[b][:, 1:2])
            k.ts("dve", hh[b][:], hh[b][:], mv[b][:, 0:1], mv[b][:, 1:2], ALU.subtract, ALU.mult)
            k.tt("pool", yb[b][:], hh[b][:], gB[:], ALU.mult)
            p = ptb[c % 2]
            k.tr(p[:], yb[b][:], identb[:])
            k.copy("act", yo[b][:], p[:])
            k.dma("sp", yT_d[256:384, cs], yo[b][:], yo[b])
    k.barrier()

import contextlib

C_ID, C_SU, C_LE, C_LT = 0, 128, 256, 384
C_BO = 512
C_HS = 640
C_LT2, C_LE2, C_GT2, C_I2 = 648, 776, 904, 1032
NCST = 1160
LN_EPS = 64e-5
import os
STAGE = int(os.environ.get("RW_STAGE", "9"))


def emit_RW(k, T, scr, cst, rwp_d, rww_d, rwln_d, yT_d, psb):
    TT = 512
    L = 64
    with contextlib.ExitStack() as st:
        sb = lambda n, s, d=F32: k.sb("RW_" + n, s, d, stack=st)
        P = sb("p", [128, 16]); k.dma("sp", P[:], rwp_d, P)
        Wt = sb("w", [128, 384]); k.dma("sp", Wt[:], rww_d, Wt)
        LNt = sb("ln", [128, 256]); k.dma("sp", LNt[:], rwln_d, LNt)
        one1 = sb("one1", [128, 1]); k.memset("dve", one1[:], 1.0)
        mhalf = sb("mhalf", [128, 1]); k.memset("dve", mhalf[:], -0.5)
        lneps = sb("lneps", [128, 1]); k.memset("dve", lneps[:], LN_EPS)
        zeros = sb("zeros", [128, TT]); k.memset("dve", zeros[:], 0.0)
        negw0 = sb("negw0", [128, 1]); k.ts("dve", negw0[:], P[:, 3:4], -1.0, None, ALU.mult)
        omka = sb("omka", [128, 1]); k.ts("dve", omka[:], P[:, 6:7], -1.0, 1.0, ALU.mult, ALU.add)
        identb = sb("identb", [128, 128], BF16); k.copy("dve", identb[:], cst[:, C_ID:C_ID + 128])
        LT2 = cst[0:64, C_LT2:C_LT2 + 128]; LE2 = cst[0:64, C_LE2:C_LE2 + 128]; GT2 = cst[0:64, C_GT2:C_GT2 + 128]
        I2 = cst[0:64, C_I2:C_I2 + 128]
        raw = [[sb(f"raw{b}{i}", [128, TT + 1]) for i in range(5)] for b in range(2)]
        xs = [[sb(f"xs{b}{i}", [128, TT]) for i in range(5)] for b in range(2)]
        dcy = [sb(f"dcy{b}", [128, TT]) for b in range(2)]
        Gi = [sb(f"Gi{b}", [128, TT]) for b in range(2)]
        rGi = [sb(f"rGi{b}", [128, TT]) for b in range(2)]
        Ge = [sb(f"Ge{b}", [128, TT]) for b in range(2)]
        av = [sb(f"av{b}", [128, TT]) for b in range(2)]
        kkn = [sb(f"kkn{b}", [128, TT]) for b in range(2)]
        k2 = [sb(f"k2{b}", [128, TT]) for b in range(2)]
        tA = [sb(f"tA{b}", [128, TT]) for b in range(2)]
        tB = [sb(f"tB{b}", [128, TT]) for b in range(2)]
        rkr = [sb(f"rkr{b}", [128, TT]) for b in range(2)]
        sg = [sb(f"sg{b}", [128, TT]) for b in range(2)]
        ab_ = [sb(f"ab{b}", [128, TT], BF16) for b in range(2)]
        bb_ = [sb(f"bb{b}", [128, TT], BF16) for b in range(2)]
        kb_ = [sb(f"kb{b}", [128, TT], BF16) for b in range(2)]
        rb_ = [sb(f"rb{b}", [128, TT], BF16) for b in range(2)]
        vb_ = [sb(f"vb{b}", [128, TT], BF16) for b in range(2)]
        abz = [sb(f"abz{b}", [128, 2, TT], BF16) for b in range(2)]
        bbz = [sb(f"bbz{b}", [128, 2, TT], BF16) for b in range(2)]
        rbz = [sb(f"rbz{b}", [128, 2, TT], BF16) for b in range(2)]
        hsel = cst[:, C_HS:C_HS + 2]
        glm = [sb(f"glm{i}", [128, 2]) for i in range(2)]
        ptb = [carve(psb[7], slice(0, 64), i * 64, 128, BF16) for i in range(2)]
        ptv = carve(psb[7], slice(0, 64), 128, 128, BF16)
        pto = carve(psb[7], slice(0, 128), 256, 64, BF16)
        H = sb("H", [128, 64]); k.memset("dve", H[:], 0.0)
        H1 = sb("H1", [128, 64])
        Hz = sb("Hz", [128, 2, 64], BF16); k.memset("dve", Hz[:], 0.0)
        nb = 2
        btok = [sb(f"btok{i}", [64, 128], BF16) for i in range(nb)]
        ktok = [sb(f"ktok{i}", [64, 128], BF16) for i in range(nb)]
        vtok = [sb(f"vtok{i}", [64, 128]) for i in range(nb)]
        vtokb = [sb(f"vtokb{i}", [64, 128], BF16) for i in range(nb)]
        Am = [sb(f"Am{i}", [64, 5, 128], BF16) for i in range(nb)]
        PQ = [sb(f"PQ{i}", [64, 256], BF16) for i in range(nb)]
        TT_ = [sb(f"TT{i}", [64, 256], BF16) for i in range(nb)]
        Wsb = [sb(f"Wsb{i}", [64, 128], BF16) for i in range(nb)]
        Usb = [sb(f"Usb{i}", [64, 128], BF16) for i in range(nb)]
        Ysb = [sb(f"Ysb{i}", [64, 128]) for i in range(nb)]
        bon = [sb(f"bon{i}", [64, 2]) for i in range(nb)]
        stats = [sb(f"stats{i}", [64, 2, 6]) for i in range(nb)]
        mv = [sb(f"mv{i}", [64, 2, 2]) for i in range(nb)]
        yb = [sb(f"yb{i}", [64, 128], BF16) for i in range(nb)]
        yo = [sb(f"yo{i}", [128, 64], BF16) for i in range(nb)]

        def prep(ti):
            b = ti % 2
            t0 = ti * TT
            for i in range(5):
                k.dma("sp", raw[b][i][:], scr.rw[i][:, t0:t0 + TT + 1], raw[b][i])
            mucol = [0, 1, 2, 11, 12]
            for i in range(5):
                e = "pool" if i % 2 else "dve"
                k.tt(e, tA[b][:], raw[b][i][:, 0:TT], raw[b][i][:, 1:TT + 1], ALU.subtract)
                k.stt(xs[b][i][:], tA[b][:], P[:, mucol[i]:mucol[i] + 1], raw[b][i][:, 1:TT + 1], ALU.mult, ALU.add)
            r_, k_, v_, wa_, gd_ = xs[b]
            k.act(wa_[0:64, :], wa_[0:64, :], AF.Tanh)
            ps = psb[0]
            k.mm(ps[:], Wt[:, 0:128], wa_[:, :])
            k.act(tA[b][:], ps[:], AF.Exp, bias=negw0[:], scale=-1.0)
            k.act(tA[b][:], tA[b][:], AF.Ln, bias=one1[:])
            k.act(tA[b][:], tA[b][:], AF.Exp, bias=mhalf[:], scale=-1.0)
            k.act(dcy[b][:], tA[b][:], AF.Exp, scale=-1.0)
            k.act(tB[b][:], tA[b][:], AF.Exp)
            for c in range(TT // L):
                cs = slice(c * L, (c + 1) * L)
                k.scan(Gi[b][:, cs], dcy[b][:, cs], zeros[:, cs], 1.0, ALU.mult, ALU.add)
            k.recip(rGi[b][:], Gi[b][:])
            k.tt("pool", Ge[b][:], Gi[b][:], tB[b][:], ALU.mult)
            ps = psb[1]
            k.mm(ps[:], Wt[:, 128:256], wa_[:, :])
            k.act(av[b][:], ps[:], AF.Sigmoid, bias=P[:, 4:5])
            k.ts("dve", tA[b][:], k_[:], P[:, 5:6], None, ALU.mult)
            k.tt("pool", tB[b][:], tA[b][:], tA[b][:], ALU.mult)
            ps = psb[2]
            k.mm(ps[:], cst[:, C_BO:C_BO + 128], tB[b][:])
            k.act(tB[b][:], ps[:], AF.Sqrt)
            k.ts("dve", tB[b][:], tB[b][:], 1e-12, None, ALU.max)
            k.recip(tB[b][:], tB[b][:])
            k.tt("dve", kkn[b][:], tA[b][:], tB[b][:], ALU.mult)
            k.ts("dve", tA[b][:], av[b][:], P[:, 6:7], omka[:], ALU.mult, ALU.add)
            k.tt("pool", k2[b][:], k_[:], tA[b][:], ALU.mult)
            k.stt(ab_[b][:], kkn[b][:], -1.0, Ge[b][:], ALU.mult, ALU.mult)
            k.tt("pool", tA[b][:], kkn[b][:], av[b][:], ALU.mult)
            k.tt("dve", bb_[b][:], tA[b][:], rGi[b][:], ALU.mult)
            k.tt("pool", kb_[b][:], k2[b][:], rGi[b][:], ALU.mult)
            k.tt("dve", rb_[b][:], r_[:], Gi[b][:], ALU.mult)
            for hh in range(2):
                k.ts("pool", abz[b][:, hh, :], ab_[b][:], hsel[:, hh:hh + 1], None, ALU.mult)
                k.ts("dve", bbz[b][:, hh, :], bb_[b][:], hsel[:, hh:hh + 1], None, ALU.mult)
                k.ts("pool", rbz[b][:, hh, :], rb_[b][:], hsel[:, hh:hh + 1], None, ALU.mult)
            k.copy("act", vb_[b][:], v_[:])
            k.tt("pool", tB[b][:], r_[:], k2[b][:], ALU.mult)
            k.ts("dve", rkr[b][:], tB[b][:], P[:, 8:9], None, ALU.mult)
            k.act(sg[b][:], gd_[:], AF.Sigmoid)

        gtok = [sb(f"gtok{i}", [64, 128]) for i in range(nb)]

        def cx(ti, c):
            b = ti % 2
            n = (ti * (TT // L) + c)
            return b, n, n % nb, ti * TT + c * L, slice(c * L, (c + 1) * L)

        def pre(ti, c):
            b, n, q, t0, cs = cx(ti, c)
            k.tr(ptb[0][:], bb_[b][:, cs], identb[:]); k.copy("act", btok[q][:], ptb[0][:])
            k.tr(ptb[1][:], kb_[b][:, cs], identb[:]); k.copy("dve", ktok[q][:], ptb[1][:])
            k.tr(ptv[:], vb_[b][:, cs], identb[:])
            k.copy("act", vtokb[q][:], ptv[:])
            pa = psb[3]; pm = psb[4]
            k.mm(pa[0:64, 0:128], bb_[b][:, cs], abz[b][:, :, cs])
            k.mm(pa[0:64, 128:256], ab_[b][:, cs], bbz[b][:, :, cs])
            k.mm(pa[0:64, 256:384], kb_[b][:, cs], abz[b][:, :, cs])
            k.mm(pa[0:64, 384:512], bb_[b][:, cs], rbz[b][:, :, cs])
            k.mm(pm[0:64, 0:128], kb_[b][:, cs], rbz[b][:, :, cs])
            k.mm(pm[0:64, 128:256], sg[b][:, cs], Wt[:, 256:384])
            k.mm(pm[0:64, 256:258], rkr[b][:, cs], cst[:, C_HS:C_HS + 2])
            A = Am[q]
            k.tt("dve", A[:, 0, :], pa[0:64, 0:128], LT2, ALU.mult)
            k.tt("dve", A[:, 1, :], pa[0:64, 128:256], GT2, ALU.mult)
            k.tt("dve", A[:, 2, :], pa[0:64, 256:384], LT2, ALU.mult)
            k.tt("dve", A[:, 3, :], pa[0:64, 384:512], LE2, ALU.mult)
            k.tt("dve", A[:, 4, :], pm[0:64, 0:128], LE2, ALU.mult)
            k.copy("act", bon[q][:], pm[0:64, 256:258])
            k.copy("act", gtok[q][:], pm[0:64, 128:256])
            k.copy("pool", PQ[q][:, 0:128], A[:, 0, :]); k.copy("pool", PQ[q][:, 128:256], A[:, 1, :])
            k.tt("dve", TT_[q][:, 0:128], A[:, 0, :], I2, ALU.add)
            k.tt("dve", TT_[q][:, 128:256], A[:, 1, :], I2, ALU.add)

        def inv(ti, c, it):
            b, n, q, t0, cs = cx(ti, c)
            pp = psb[5]
            for hh in range(2):
                o = hh * 64
                k.mm(pp[0:64, o:o + 64], PQ[q][:, 128 + o:128 + o + 64], PQ[q][:, o:o + 64])
                k.mm(pp[0:64, 128 + o:128 + o + 64], PQ[q][:, o:o + 64], PQ[q][:, 128 + o:128 + o + 64])
            k.copy("act", PQ[q][:], pp[0:64, 0:256])
            for hh in range(2):
                o = hh * 64
                k.mm(pp[0:64, 256 + o:256 + o + 64], TT_[q][:, 128 + o:128 + o + 64], PQ[q][:, o:o + 64])
                k.mm(pp[0:64, 384 + o:384 + o + 64], TT_[q][:, o:o + 64], PQ[q][:, 128 + o:128 + o + 64])
            k.tt("dve", TT_[q][:], TT_[q][:], pp[0:64, 256:512], ALU.add)

        def ch(ti, c, step):
            b, n, q, t0, cs = cx(ti, c)
            A = Am[q]
            pw = psb[6]
            if step == 0:
                k.mm(pw[0:64, 0:128], ab_[b][:, cs], Hz[:], start=True, stop=False, skip_group_check=True)
                for hh in range(2):
                    o = hh * 64
                    k.mm(pw[0:64, o:o + 64], A[:, 2, o:o + 64], vtokb[q][:, o:o + 64], start=False, stop=True, skip_group_check=True)
                k.copy("act", Wsb[q][:], pw[0:64, 0:128])
            elif step == 1:
                for hh in range(2):
                    o = hh * 64
                    k.mm(pw[0:64, 128 + o:128 + o + 64], TT_[q][:, o:o + 64], Wsb[q][:, o:o + 64], skip_group_check=True)
                k.copy("act", Usb[q][:], pw[0:64, 128:256])
            elif step == 2:
                k.mm(pw[0:64, 256:384], rb_[b][:, cs], Hz[:], start=True, stop=False, skip_group_check=True)
                for hh in range(2):
                    o = hh * 64
                    k.mm(pw[0:64, 256 + o:256 + o + 64], A[:, 3, o:o + 64], Usb[q][:, o:o + 64], start=False, stop=False, skip_group_check=True)
                    k.mm(pw[0:64, 256 + o:256 + o + 64], A[:, 4, o:o + 64], vtokb[q][:, o:o + 64], start=False, stop=True, skip_group_check=True)
                pd = psb[(n % 2)]
                k.mm(pd[:, 0:128], btok[q][:], Usb[q][:], start=True, stop=False)
                k.mm(pd[:, 0:128], ktok[q][:], vtokb[q][:], start=False, stop=True)
                for hh in range(2):
                    hs = slice(hh * 64, hh * 64 + 64)
                    k.tt("dve", H1[hs, :], H[hs, :], pd[hs, hh * 64:hh * 64 + 64], ALU.add)
                gl = Gi[b][:, c * L + L - 1:c * L + L]
                g2 = glm[n % 2]
                k.ts("pool", g2[:], hsel, gl, None, ALU.mult)
                k.ts("dve", H[:], H1[:], gl, None, ALU.mult)
                for hh in range(2):
                    k.act(Hz[:, hh, :], H1[:], AF.Copy, scale=g2[:, hh:hh + 1])
                k.copy("act", Ysb[q][:], pw[0:64, 256:384])
            elif step == 3:
                for hh in range(2):
                    o = hh * 64
                    k.op("dve", lambda e, hh=hh, o=o: e.bn_stats(stats[q].h[:, hh, :], Ysb[q].h[:, o:o + 64]), [Ysb[q][:]], [stats[q][:]])
                    k.op("dve", lambda e, hh=hh: e.bn_aggr(mv[q].h[:, hh, :], stats[q].h[:, hh, :]), [stats[q][:]], [mv[q][:]])
                k.act(mv[q][:, :, 1], mv[q][:, :, 1], AF.Sqrt, bias=lneps[0:64, :])
                k.recip(mv[q][:, :, 1], mv[q][:, :, 1])
                for hh in range(2):
                    o = hh * 64
                    k.ts("dve", Ysb[q][:, o:o + 64], Ysb[q][:, o:o + 64], mv[q][:, hh, 0:1], mv[q][:, hh, 1:2], ALU.subtract, ALU.mult)
            else:
                k.tt("pool", Ysb[q][:], Ysb[q][:], LNt[0:64, 0:128], ALU.mult)
                k.tt("pool", Ysb[q][:], Ysb[q][:], LNt[0:64, 128:256], ALU.add)
                for hh in range(2):
                    o = hh * 64
                    k.stt(Ysb[q][:, o:o + 64], vtokb[q][:, o:o + 64], bon[q][:, hh:hh + 1], Ysb[q][:, o:o + 64], ALU.mult, ALU.add)
                k.tt("dve", yb[q][:], Ysb[q][:], gtok[q][:], ALU.mult)
                k.tr(pto[:], yb[q][:], identb[0:64, 0:64])
                k.copy("act", yo[q][:], pto[:])
                k.dma("sp", yT_d[384:512, t0:t0 + L], yo[q][:], yo[q])

        NTI = T // TT
        NCH = TT // L
        seq = [(ti, c) for ti in range(NTI) for c in range(NCH)]
        prep(0)
        if NTI > 1:
            prep(1)
        pre(*seq[0])
        for it in range(5):
            inv(*seq[0], it)
        for idx, (ti, c) in enumerate(seq):
            if c == 0 and ti >= 1 and ti + 1 < NTI:
                prep(ti + 1)
            nxt = seq[idx + 1] if idx + 1 < len(seq) else None
            if nxt is not None:
                pre(*nxt)
            for step in range(5):
                if nxt is not None:
                    inv(*nxt, step)
                ch(ti, c, step)
    k.barrier()

import contextlib, math
import numpy as np

C_ID = 0
NEG = -30000.0


def rel_bucket_np(d):
    n = np.maximum(d, 0)
    nf = np.maximum(n, 16).astype(np.float32)
    large = 16 + (np.log(nf / np.float32(16)) / np.float32(math.log(8.0)) * np.float32(16)).astype(np.int32)
    return np.where(n < 16, n, np.minimum(large, 31))


def cg_of(gi):
    return (gi % 2) * 2 + gi // 2


def nsa_tables(rel_bias, g, T):
    n_blk = T // 64
    tab = np.asarray(rel_bias, np.float32)
    p = np.arange(128)[:, None]; i = np.arange(128)[None, :]
    ds = []
    vals = []
    for di in range(18):
        d = 128 * di + i - 16 * p - 31
        ds.append(d); vals.append(d >= 0)
    for o in range(2):
        d = 128 * o + i - p
        ds.append(d); vals.append(d >= 0)
    for o in range(5):
        d = 128 * o + i - p
        ds.append(d); vals.append((d >= 0) & (d < 512))
    gath = np.zeros((25, 128, 4, 128), np.float32)
    val = np.zeros((25, 128, 4, 128), np.float32)
    cfar = np.zeros((128, 4, 128), np.float32)
    for gi in range(4):
        cg = cg_of(gi)
        row = tab[g * 4 + gi]
        for x in range(25):
            gath[x, :, cg, :] = row[rel_bucket_np(ds[x])]
            val[x, :, cg, :] = vals[x]
        cfar[:, cg, :] = row[31]
    ii = np.arange(128)[:, None]; c = np.arange(2 * n_blk)[None, :]
    rel = c - n_blk; cur = ii // 64
    Gt = np.where((rel == cur) | (rel == cur - 1), 1e30, np.where(rel > cur, -1e30, 0.0)).astype(np.float32)
    pp = np.arange(128)[:, None]; cc = np.arange(34)[None, :]
    wloc = ((4 * cc - 1 <= pp) & (pp <= 4 * cc + 3)).astype(np.float32)
    r = np.arange(128)[:, None]; c2 = np.arange(64 * 128)[None, :]
    stair = (c2 // 64 == r).astype(np.float32)
    return (gath.reshape(25, 128, 512), val.reshape(25, 128, 512), cfar.reshape(128, 512), Gt, wloc, stair)


def emit_NSA(k, T, scr, cst, din, yT_d, psb):
    NQB = T // 128
    n_cmp = T // 16
    n_blk = T // 64
    NCT = (n_cmp + 127) // 128
    NBH = (n_blk + 127) // 128
    with contextlib.ExitStack() as st:
        sb = lambda n, s, d=F32: k.sb("NS_" + n, s, d, stack=st)
        kcT2 = sb("kcT2", [128, NCT * 128], BF16)
        vcmp = sb("vcmp", [128, NCT, 66], BF16)
        k.memset("dve", kcT2[:], 0.0)
        k.memset("dve", vcmp[:], 0.0)
        k.memset("dve", vcmp[:, :, 64:65], 1.0)
        with contextlib.ExitStack() as st2:
            sb2 = lambda n, s, d=F32: k.sb("NC_" + n, s, d, stack=st2)
            for which in range(2):
                X = sb2(f"X{which}", [64, T + 16], BF16)
                k.dma("sp", X[:], scr.kcvc[which * 64:(which + 1) * 64, :], X)
                w1 = sb2(f"w1{which}", [64, 32, 128], BF16)
                k.dma("pool", w1[:], din["w1"][which].rearrange("(l d) h -> d l h", d=64), w1)
                w2 = sb2(f"w2{which}", [128, 128], BF16)
                k.dma("pool", w2[:], din["w2d"][which], w2)
                posf = sb2(f"posf{which}", [64, 32]); k.dma("sp", posf[:], din["posT"][which], posf)
                posb = sb2(f"posb{which}", [64, 32], BF16); k.copy("dve", posb[:], posf[:])
                ps = psb[0]
                for l in range(32):
                    k.mm(ps[:, 0:1], w1[:, l, :], posb[:, l:l + 1], start=(l == 0), stop=(l == 31))
                hb = sb2(f"hb{which}", [128, 1]); k.copy("dve", hb[:], ps[:, 0:1])
                hid = sb2(f"hid{which}", [128, NCT * 128], BF16)
                k.memset("dve", hid[:], 0.0)
                for n0 in range(0, n_cmp, 512):
                    nn = min(512, n_cmp - n0)
                    ps = psb[1 + (n0 // 512) % 2]
                    for l in range(32):
                        a = l + 16 * n0
                        k.mm(ps[:, 0:nn], w1[:, l, :], X[:, a:a + 16 * (nn - 1) + 1:16], start=(l == 0), stop=(l == 31))
                    k.act(hid[:, n0:n0 + nn], ps[:, 0:nn], AF.Silu, bias=hb[:])
                if which == 0:
                    for n0 in range(0, n_cmp, 512):
                        nn = min(512, n_cmp - n0)
                        ps = psb[3 + (n0 // 512) % 2]
                        k.mm(ps[:, 0:nn], w2[:], hid[:, n0:n0 + nn])
                        k.copy("dve", kcT2[:, n0:n0 + nn], ps[:, 0:nn])
                else:
                    for nt in range(NCT):
                        ps = psb[5 + nt % 2]
                        k.mm(ps[:, 0:64], hid[:, nt * 128:(nt + 1) * 128], w2[:, 0:64])
                        k.copy("dve", vcmp[:, nt, 0:64], ps[:, 0:64])
                if n_cmp % 128:
                    pass
        k.barrier()
        ksT2 = sb("ksT2", [128, T], BF16); kwT2 = sb("kwT2", [128, T], BF16)
        for hf in range(2):
            k.dma("sp", ksT2[hf * 64:(hf + 1) * 64, :], scr.kskw[0:64, :], ksT2)
            k.dma("sp", kwT2[hf * 64:(hf + 1) * 64, :], scr.kskw[64:128, :], kwT2)
        vs = sb("vs", [128, NQB, 66], BF16); vw = sb("vw", [128, NQB, 66], BF16)
        k.memset("dve", vs[:, :, 64:66], 1.0); k.memset("dve", vw[:, :, 64:66], 1.0)
        tmr = scr.tmb.rearrange("(c s) n -> s c n", s=128)
        k.dma("sp", vs[:, :, 0:64], tmr[:, :, 0:64], vs)
        k.dma("sp", vw[:, :, 0:64], tmr[:, :, 64:128], vw)
        stair = sb("stair", [128, 64 * 128], BF16)
        k.dma("pool", stair[:, 0:4096], din["stair"][:, 0:4096], stair)
        k.dma("pool", stair[:, 4096:8192], din["stair"][:, 4096:8192], stair)
        wloc = sb("wloc", [128, 34], BF16); k.dma("pool", wloc[:], din["wloc"], wloc)
        Gt = sb("Gt", [128, 2 * n_blk]); k.dma("sp", Gt[:], din["Gt"], Gt)
        tabs = sb("tabs", [128, 25, 512], BF16)
        cfar = sb("cfar", [128, 512]); k.dma("sp", cfar[:], din["cfar"], cfar)
        tg = [sb(f"tg{i}", [128, 512]) for i in range(2)]
        tv = [sb(f"tv{i}", [128, 512]) for i in range(2)]
        for x in range(25):
            a = tg[x % 2]; v_ = tv[x % 2]
            k.dma("sp", a[:], din["gath"][x], a)
            k.dma("sp", v_[:], din["val"][x], v_)
            k.tt("dve", a[:], a[:], cfar[:], ALU.subtract)
            k.act(a[:], a[:], AF.Exp)
            k.tt("dve", tabs[:, x, :], a[:], v_[:], ALU.mult)
        identb = sb("identb", [128, 128], BF16); k.copy("dve", identb[:], cst[:, C_ID:C_ID + 128])
        q2 = [sb(f"q2{i}", [128, 2, 2, 128], BF16) for i in range(2)]
        for i in range(2):
            k.memset("dve", q2[i][:], 0.0)
        gts = [sb(f"gts{i}", [128, 12]) for i in range(2)]
        E = [sb(f"E{i}", [128, 512]) for i in range(3)]
        PT = [sb(f"PT{i}", [128, 512], BF16) for i in range(3)]
        osum = [sb(f"osum{i}", [128, 4, 64]) for i in range(2)]
        coef = [sb(f"coef{i}", [128, 4]) for i in range(3)]
        rdn = [sb(f"rdn{i}", [128, 4]) for i in range(3)]
        score = sb("score", [128, n_blk]); sc2 = sb("sc2", [128, n_blk])
        m8 = sb("m8", [128, 8]); m8b = sb("m8b", [128, 8])
        nsel = sb("nsel", [128, NBH * 128]); k.memset("dve", nsel[:], 0.0)
        nselT = [sb(f"nselT{i}", [128, NBH, 4, 128], BF16) for i in range(2)]
        yb = [sb(f"yb{i}", [128, 256], BF16) for i in range(2)]
        yo = [sb(f"yo{i}", [128, 2, 128], BF16) for i in range(2)]
        ptr = [carve(psb[7], slice(0, 128), i * 128, 128, F32) for i in range(2)]
        pto = [carve(psb[7], slice(0, 128), 256 + i * 64, 128, BF16) for i in range(2)]
        qr = scr.q.rearrange("(j hh d) t -> hh d j t", hh=2, d=64)
        ei = 0
        si = 0

        def scores(kT2, ktile, q2f):
            nonlocal si
            ps = psb[si % 2]; si += 1
            k.mm(ps[:, :], kT2[:, ktile * 128:(ktile + 1) * 128], q2f[:], start=True, stop=True, skip_group_check=True)
            return ps

        def softmax_piece(ps, tabx):
            nonlocal ei
            pt = PT[ei % 3]; e = E[ei % 3]; ei += 1
            if tabx is None:
                k.act(pt[:], ps[:], AF.Exp)
            else:
                k.act(e[:], ps[:], AF.Exp)
                k.tt("dve", pt[:], e[:], tabs[:, tabx, :], ALU.mult)
            return pt


        def stageA(qb):
            nonlocal si, ei
            b = qb % 2
            t0 = qb * 128
            for hh in range(2):
                k.dma("sp", q2[b][hh * 64:(hh + 1) * 64, hh, :, :], qr[hh][:, :, t0:t0 + 128], q2[b])
            k.dma("sp", gts[b][:], scr.tmg[t0:t0 + 128, :], gts[b])
            k.act(gts[b][:], gts[b][:], AF.Sigmoid)
            q2f = q2[b]

            nct = min(NCT, (8 * qb + 7 + 127) // 128)
            po = psb[2]; pi0 = psb[3]; pi1 = psb[4]
            first = True
            def pv_cmp(pt, nt):
                ncol = min(33, n_blk - 32 * nt)
                for gi in range(4):
                    cg = cg_of(gi)
                    k.mm(po[:, gi * 66:gi * 66 + 66], pt[:, cg * 128:(cg + 1) * 128], vcmp[:, nt, :], start=(nt == 0 and gi == 0), stop=True, skip_group_check=True)
                    pim = pi0 if gi < 2 else pi1
                    c0 = (gi % 2) * 256 + 32 * nt
                    k.mm(pim[:, c0:c0 + ncol], pt[:, cg * 128:(cg + 1) * 128], wloc[:, 0:ncol], start=(nt == 0 and gi in (0, 2)), stop=True, skip_group_check=True)
            pend = None
            for nt in range(nct):
                ps = scores(kcT2, nt, q2f)
                if pend is not None:
                    pv_cmp(*pend)
                di = qb - 16 * nt
                pt = softmax_piece(ps, di if di < 18 else None)
                pend = (pt, nt)
            pv_cmp(*pend)
            r = rdn[0]; cf = coef[0]
            k.ts("dve", r[:], po[:, 0:264].re("p (g c) -> p g c", c=66)[:, :, 64], 1e-30, None, ALU.max)
            k.recip(r[:], r[:])
            k.tt("dve", cf[:], r[:], gts[b][:].re("p (g c) -> p g c", c=3)[:, :, 0], ALU.mult)
            for gi in range(4):
                k.ts("dve", osum[b][:, gi, :], po[:, gi * 66:gi * 66 + 64], cf[:, gi:gi + 1], None, ALU.mult)
            nb_valid = min(n_blk, 32 * nct + 1) if nct < NCT else n_blk
            k.memset("pool", score[:], 0.0)
            W_ = min(n_blk, 32 * (nct - 1) + 33)
            k.ts("dve", score[:, 0:W_], pi0[:, 0:W_], r[:, 0:1], None, ALU.mult)
            k.stt(score[:, 0:W_], pi0[:, 256:256 + W_], r[:, 1:2], score[:, 0:W_], ALU.mult, ALU.add)
            k.stt(score[:, 0:W_], pi1[:, 0:W_], r[:, 2:3], score[:, 0:W_], ALU.mult, ALU.add)
            k.stt(score[:, 0:W_], pi1[:, 256:256 + W_], r[:, 3:4], score[:, 0:W_], ALU.mult, ALU.add)
            k.tt("dve", score[:], score[:], Gt[:, n_blk - 2 * qb:2 * n_blk - 2 * qb], ALU.add)
            k.memset("dve", score[:, 0:1], 1e30)
            k.max8(m8[:], score[:])
            k.match_replace(sc2[:], m8[:], score[:], -3e38)
            k.max8(m8b[:], sc2[:])
            k.ts("dve", nsel[:, 0:n_blk], score[:], m8b[:, 7:8], None, ALU.is_ge)
            k.ts("dve", nsel[:, 0:n_blk], nsel[:, 0:n_blk], -NEG, NEG, ALU.mult, ALU.add)
            for hf in range(NBH):
                k.tr(ptr[hf][:], nsel[:, hf * 128:(hf + 1) * 128], cst[:, C_ID:C_ID + 128])
                for cg in range(4):
                    k.copy("act" if cg % 2 else "dve", nselT[b][:, hf, cg, :], ptr[hf][:])
        def stageB(qb):
            nonlocal si, ei
            b = qb % 2
            t0 = qb * 128
            q2f = q2[b]
            po = psb[5]
            def pv_slc(pt, kt):
                for gi in range(4):
                    cg = cg_of(gi)
                    k.mm(po[:, gi * 66:gi * 66 + 66], pt[:, cg * 128:(cg + 1) * 128], vs[:, kt, :], start=(kt == 0 and gi == 0), stop=True, skip_group_check=True)
            pend = None
            for kt in range(qb + 1):
                hf = (2 * kt) // 128
                ps = psb[si % 2]; si += 1
                k.mm(ps[:, :], ksT2[:, kt * 128:(kt + 1) * 128], q2f[:], start=True, stop=False, skip_group_check=True)
                c0 = 128 * kt - 8192 * hf
                k.mm(ps[:, :], stair[:, c0:c0 + 128], nselT[b][:, hf, :, :], start=False, stop=True, skip_group_check=True)
                if pend is not None:
                    pv_slc(*pend)
                o = qb - kt
                pt = softmax_piece(ps, 18 + o if o < 2 else None)
                pend = (pt, kt)
            pv_slc(*pend)
            r = rdn[1]; cf = coef[1]
            k.recip(r[:], po[:, 0:264].re("p (g c) -> p g c", c=66)[:, :, 64])
            k.tt("dve", cf[:], r[:], gts[b][:].re("p (g c) -> p g c", c=3)[:, :, 1], ALU.mult)
            for gi in range(4):
                k.stt(osum[b][:, gi, :], po[:, gi * 66:gi * 66 + 64], cf[:, gi:gi + 1], osum[b][:, gi, :], ALU.mult, ALU.add)
            po = psb[6]
            kts = list(range(max(0, qb - 4), qb + 1))
            def pv_win(pt, kt):
                for gi in range(4):
                    cg = cg_of(gi)
                    k.mm(po[:, gi * 66:gi * 66 + 66], pt[:, cg * 128:(cg + 1) * 128], vw[:, kt, :], start=(kt == kts[0] and gi == 0), stop=True, skip_group_check=True)
            pend = None
            for kt in kts:
                ps = scores(kwT2, kt, q2f)
                if pend is not None:
                    pv_win(*pend)
                pt = softmax_piece(ps, 20 + (qb - kt))
                pend = (pt, kt)
            pv_win(*pend)
            r = rdn[2]; cf = coef[2]
            k.recip(r[:], po[:, 0:264].re("p (g c) -> p g c", c=66)[:, :, 64])
            k.tt("dve", cf[:], r[:], gts[b][:].re("p (g c) -> p g c", c=3)[:, :, 2], ALU.mult)
            for gi in range(4):
                k.stt(yb[b][:, gi * 64:(gi + 1) * 64], po[:, gi * 66:gi * 66 + 64], cf[:, gi:gi + 1], osum[b][:, gi, :], ALU.mult, ALU.add)
            for j in range(2):
                k.tr(pto[j][:], yb[b][:, j * 128:(j + 1) * 128], identb[:])
                k.copy("act", yo[b][:, j, :], pto[j][:])
            k.dma("sp", yT_d[0:256, t0:t0 + 128].rearrange("(j p) t -> p j t", p=128), yo[b][:], yo[b])
        stageA(0)
        for qb in range(NQB):
            if qb + 1 < NQB:
                stageA(qb + 1)
            stageB(qb)
    k.barrier()

import ml_dtypes
from concourse.bass_utils import run_bass_kernel_spmd

NCORE = 8
SEQ = 16384
NT = 4096


def consts_np():
    c = np.zeros((128, NCST), np.float32)
    i = np.arange(128)
    c[:, 0:128] = np.eye(128); c[:, 128:256] = (i[:, None] < i[None, :]); c[:, 256:384] = (i[:, None] <= i[None, :]); c[:, 384:512] = (i[:, None] < i[None, :])
    c[:, 512:640] = (i[:, None] // 64 == i[None, :] // 64)
    c[:, 640] = (i < 64); c[:, 641] = (i >= 64)
    s = np.arange(64)[:, None]; t = np.arange(128)[None, :] % 64
    c[0:64, 648:776] = (s < t); c[0:64, 776:904] = (s <= t); c[0:64, 904:1032] = (s > t); c[0:64, 1032:1160] = (s == t)
    return c


def lay(v):
    return np.ascontiguousarray(np.asarray(v, np.float32).reshape(-1, 128).T)


_prog_cache = {}


def build_R(kind):
    if kind in _prog_cache:
        return _prog_cache[kind]
    k = KB()
    nblk = {"first": 5, "mid": 9, "last": 4}[kind]
    nnorm = {"first": 2, "mid": 3, "last": 1}[kind]
    xT_d = k.din("xT", [DC, 128, NT]); condc = k.din("condc", [128, 16])
    adaw = k.din("adaw", [D, nblk * D]); adab = k.din("adab", [128, nblk * 16]); ng = k.din("ng", [128, nnorm * 16])
    nffn = 2 if kind == "mid" else 1
    wgs = [k.din(f"wg{j}", [D, DFF]) for j in range(nffn)]
    wus = [k.din(f"wu{j}", [D, DFF]) for j in range(nffn)]
    wds = [k.din(f"wd{j}", [DFF, D]) for j in range(nffn)]
    if kind != "first":
        yT_d = k.din("yT", [DC, 128, NT], BF16); wo_d = k.din("wo", [D, D])
    if kind == "last":
        fng_d = k.din("fng", [128, 16])
        out_d = k.dout("outT", [DC, 128, NT])
    else:
        xo = k.dout("xo", [DC, 128, NT]); ho = k.dout("ho", [DC, 128, NT], BF16)
    rl = RL(k, NT)
    cT = k.sb("cT", [128, 16]); abt = k.sb("abt", [128, nblk * 16]); ngt = k.sb("ngt", [128, nnorm * 16])
    k.dma("sp", cT[:], condc, cT); k.dma("sp", abt[:], adab, abt); k.dma("sp", ngt[:], ng, ngt)
    k.act(cT[:], cT[:], AF.Silu)
    md = rl.mod("md", cT[:], adaw, abt[:], list(range(nblk)))
    sm = lambda name: k.sb("s_" + name, [128, 16])
    def Gmul(name, blk, ni):
        t = sm(name); k.stt(t[:], md[:, blk, :], 1.0, ngt[:, ni * 16:(ni + 1) * 16], ALU.add, ALU.mult); return t
    def half(name, blk):
        t = sm(name); k.ts("dve", t[:], md[:, blk, :], 0.5, None, ALU.mult); return t
    if kind == "first":
        G0 = Gmul("G0", 1, 0); G1 = Gmul("G1", 4, 1); g0 = half("g0", 2); sh0 = md[:, 0, :]; sh1 = md[:, 3, :]
    elif kind == "mid":
        G2 = Gmul("G2", 2, 0); g2 = half("g2", 3); sh2 = md[:, 1, :]; gm = md[:, 0, :]
        G0 = Gmul("G0", 5, 1); G1 = Gmul("G1", 8, 2); g0 = half("g0", 6); sh0 = md[:, 4, :]; sh1 = md[:, 7, :]
    else:
        G2 = Gmul("G2", 2, 0); g2 = half("g2", 3); sh2 = md[:, 1, :]; gm = md[:, 0, :]
        fng = sm("fng"); k.dma("sp", fng[:], fng_d, fng)
    k.barrier(); rl.alloc()
    for t0 in range(0, NT, TT):
        rl.load_x(xT_d, t0)
        j = 0
        if kind != "first":
            rl.mix(yT_d, t0, wo_d, gm)
            rl.adaln(G2[:], sh2)
            rl.ffn(wgs[0], wus[0], wds[0], g2[:]); j = 1
        if kind != "last":
            rl.adaln(G0[:], sh0)
            rl.ffn(wgs[j], wus[j], wds[j], g0[:])
            rl.store_x(xo, t0)
            rl.adaln(G1[:], sh1)
            k.dma("sp", ho.rearrange("c p t -> p c t")[:, :, t0:t0 + TT], rl.ht[:], rl.ht)
        else:
            rl.rstd_calc()
            for c in range(DC):
                k.tt("pool" if c % 2 else "dve", rl.xt[:, c, :], rl.xt[:, c, :], rl.rstd[:], ALU.mult)
                k.ts("dve", rl.xt[:, c, :], rl.xt[:, c, :], fng[:, c:c + 1], None, ALU.mult)
            rl.store_x(out_d, t0)
    k.finish()
    _prog_cache[kind] = k
    return k


def build_M():
    if "M" in _prog_cache:
        return _prog_cache["M"]
    T = SEQ
    k = KB()
    hT_d = k.din("hT", [16, 128, T], BF16); w_d = k.din("w", [2048, NCOL]); cst_d = k.din("cst", [128, NCST])
    mlp_d = k.din("mlp", [128, 12]); mlg_d = k.din("mlg", [128, 128])
    rwp_d = k.din("rwp", [128, 16]); rww_d = k.din("rww", [128, 384]); rwln_d = k.din("rwln", [128, 256])
    n_blk = T // 64
    nd = {"gath": k.din("gath", [25, 128, 512]), "val": k.din("val", [25, 128, 512]), "cfar": k.din("cfar", [128, 512]),
          "Gt": k.din("Gt", [128, 2 * n_blk]), "wloc": k.din("wloc", [128, 34]), "stair": k.din("stair", [128, 8192]),
          "w1": k.din("w1", [2, 2048, 128]), "w2d": k.din("w2d", [2, 128, 128]), "posT": k.din("posT", [2, 64, 32])}
    yT_d = k.dout("yT", [512, T], BF16)
    psb = [k.ps(f"psb{i}", [128, 512], F32) for i in range(8)]
    cst = k.sb("cst_sb", [128, NCST]); k.dma("sp", cst[:], cst_d, cst)
    scr = Scr(k, T)
    emit_P(k, T, hT_d, w_d, scr, psb)
    emit_ML(k, T, scr, cst[:], mlp_d, mlg_d, yT_d, psb)
    emit_RW(k, T, scr, cst[:], rwp_d, rww_d, rwln_d, yT_d, psb)
    emit_NSA(k, T, scr, cst[:], nd, yT_d, psb)
    k.finish()
    _prog_cache["M"] = k
    return k


def m_inputs(inp, l, g, hT_b, cst):
    sl = slice(g * 128, (g + 1) * 128)
    mlp = np.zeros((128, 12), np.float32)
    cw = inp["ml_conv_w"][l]; cb = inp["ml_conv_b"][l]
    mlp[:, 0:4] = cw[:, sl].T; mlp[:, 4] = cb[sl]
    mlp[:, 5:9] = cw[:, 512 + g * 128:512 + (g + 1) * 128].T; mlp[:, 9] = cb[512 + g * 128:512 + (g + 1) * 128]
    mlp[:, 10] = inp["ml_gate_b"][l, 0, g]; mlp[:, 11] = inp["ml_gate_b"][l, 1, g]
    mlg = np.ascontiguousarray(np.tile(inp["ml_norm"][l][sl][None, :], (128, 1)).astype(np.float32))
    rwp = np.zeros((128, 16), np.float32)
    mu = inp["rw_mu"][l]
    rwp[:, 0] = mu[0:512][sl]; rwp[:, 1] = mu[512:1024][sl]; rwp[:, 2] = mu[1024:1536][sl]
    rwp[:, 3] = inp["rw_w0"][l][sl]; rwp[:, 4] = inp["rw_a0"][l][sl]; rwp[:, 5] = inp["rw_k_k"][l][sl]; rwp[:, 6] = inp["rw_k_a"][l][sl]
    rwp[:, 8] = inp["rw_r_k"][l].reshape(-1)[sl]; rwp[:, 11] = mu[1536:1664]; rwp[:, 12] = mu[1664:1792]
    rww = np.zeros((128, 384), np.float32)
    rww[0:64, 0:128] = inp["rw_w_up"][l][:, sl]; rww[64:128, 128:256] = inp["rw_a_up"][l][:, sl]; rww[:, 256:384] = inp["rw_g_up"][l][:, sl]
    rwln = np.zeros((128, 256), np.float32)
    rwln[:, 0:128] = inp["rw_ln"][l, 0][sl][None, :]; rwln[:, 128:256] = inp["rw_ln"][l, 1][sl][None, :]
    gath, val, cfar, Gt, wloc, stair = nsa_tables(inp["rel_bias"], g, SEQ)
    w2 = inp["cmp_w2"][l]
    w2d = np.zeros((2, 128, 128), np.float32); w2d[0, :, 0:64] = w2[0]; w2d[0, :, 64:128] = w2[0]; w2d[1, :, 0:64] = w2[1]
    posT = np.ascontiguousarray(inp["cmp_pos"][l].transpose(0, 2, 1))
    return {"hT": hT_b, "w": np.ascontiguousarray(inp["w_in"][l][:, core_cols(g)]), "cst": cst, "mlp": mlp, "mlg": mlg,
            "rwp": rwp, "rww": rww, "rwln": rwln, "gath": gath, "val": val, "cfar": cfar, "Gt": Gt, "wloc": wloc,
            "stair": stair, "w1": np.ascontiguousarray(inp["cmp_w1"][l]), "w2d": w2d, "posT": posT}


def ada_blocks(aw, ab, blks):
    w = np.ascontiguousarray(np.concatenate([aw[:, b * D:(b + 1) * D] for b in blks], axis=1))
    bb = np.concatenate([lay(ab[b * D:(b + 1) * D]) for b in blks], axis=1)
    return w, np.ascontiguousarray(bb)


def gather_h(res, key):
    out = []
    for b in range(2):
        out.append(np.ascontiguousarray(np.concatenate([res[b * 4 + q][key] for q in range(4)], axis=2)))
    return out


def scatter_y(resM):
    outs = []
    for b in range(2):
        full = np.zeros((2048, SEQ), dtype=resM[0]["yT"].dtype)
        for g in range(4):
            y = resM[b * 4 + g]["yT"]
            full[g * 256:(g + 1) * 256] = y[0:256]
            full[1024 + g * 128:1024 + (g + 1) * 128] = y[256:384]
            full[1536 + g * 128:1536 + (g + 1) * 128] = y[384:512]
        for q in range(4):
            outs.append(np.ascontiguousarray(full[:, q * NT:(q + 1) * NT].reshape(16, 128, NT)))
    return outs


def kernel(**inputs):
    inp = {kk: np.asarray(v) for kk, v in inputs.items()}
    x = inp["x"].astype(np.float32); c = inp["c"].astype(np.float32)
    cst = consts_np()
    cores = list(range(NCORE))
    xT = [np.ascontiguousarray(x[cc // 4, (cc % 4) * NT:(cc % 4 + 1) * NT, :].T.reshape(16, 128, NT)) for cc in cores]
    condc = [lay(c[cc // 4]) for cc in cores]
    aw = inp["ada_w"]; ab = inp["ada_b"]; ngs = inp["norm_g"]
    fw = lambda name, l, j: np.ascontiguousarray(inp[name][l, j])
    kR = build_R("first")
    w, bb = ada_blocks(aw[0], ab[0], [0, 1, 2, 3, 4])
    ng = np.concatenate([lay(ngs[0, 0]), lay(ngs[0, 1])], axis=1)
    common = {"adaw": w, "adab": bb, "ng": np.ascontiguousarray(ng), "wg0": fw("ffn_w_gate", 0, 0), "wu0": fw("ffn_w_up", 0, 0), "wd0": fw("ffn_w_down", 0, 0)}
    res = run_bass_kernel_spmd(kR.nc, [dict(common, xT=xT[cc], condc=condc[cc]) for cc in cores], core_ids=cores).results
    xcur = [r["xo"] for r in res]
    hb = gather_h(res, "ho")
    for l in range(2):
        kM = build_M()
        resM = run_bass_kernel_spmd(kM.nc, [m_inputs(inp, l, cc % 4, hb[cc // 4], cst) for cc in cores], core_ids=cores).results
        ys = scatter_y(resM)
        if l == 0:
            kR = build_R("mid")
            w1_, b1_ = ada_blocks(aw[0], ab[0], [5, 6, 7, 8]); w2_, b2_ = ada_blocks(aw[1], ab[1], [0, 1, 2, 3, 4])
            ng = np.concatenate([lay(ngs[0, 2]), lay(ngs[1, 0]), lay(ngs[1, 1])], axis=1)
            common = {"adaw": np.ascontiguousarray(np.concatenate([w1_, w2_], axis=1)), "adab": np.ascontiguousarray(np.concatenate([b1_, b2_], axis=1)),
                      "ng": np.ascontiguousarray(ng), "wo": np.ascontiguousarray(inp["w_out"][0]),
                      "wg0": fw("ffn_w_gate", 0, 1), "wu0": fw("ffn_w_up", 0, 1), "wd0": fw("ffn_w_down", 0, 1),
                      "wg1": fw("ffn_w_gate", 1, 0), "wu1": fw("ffn_w_up", 1, 0), "wd1": fw("ffn_w_down", 1, 0)}
            res = run_bass_kernel_spmd(kR.nc, [dict(common, xT=xcur[cc], condc=condc[cc], yT=ys[cc]) for cc in cores], core_ids=cores).results
            xcur = [r["xo"] for r in res]
            hb = gather_h(res, "ho")
        else:
            kR = build_R("last")
            w1_, b1_ = ada_blocks(aw[1], ab[1], [5, 6, 7, 8])
            common = {"adaw": w1_, "adab": b1_, "ng": lay(ngs[1, 2]), "wo": np.ascontiguousarray(inp["w_out"][1]), "fng": lay(inp["final_norm"]),
                      "wg0": fw("ffn_w_gate", 1, 1), "wu0": fw("ffn_w_up", 1, 1), "wd0": fw("ffn_w_down", 1, 1)}
            res = run_bass_kernel_spmd(kR.nc, [dict(common, xT=xcur[cc], condc=condc[cc], yT=ys[cc]) for cc in cores], core_ids=cores).results
    out = np.zeros((2, SEQ, D), np.float32)
    for cc in cores:
        out[cc // 4, (cc % 4) * NT:(cc % 4 + 1) * NT, :] = res[cc]["outT"].reshape(D, NT).T
    return out
```
